# Optimizing a Trainium2 kernel written in Bass

```python
import jax, jax.numpy as jnp
from jax import lax
import numpy as np

D_MODEL = 1024
BATCH = 8
SEQ = 2048
DEPTH = 2
DEC_BATCH = 128
DEC_SEQ = 1
PAST_LEN = 16384
PAGE_SIZE = 128

D_A = D_MODEL
N_GROUPS_A = 4
CHUNK = 128
D_B = D_MODEL
CONV_W = 3
D_C = D_MODEL
POOL_WINDOWS = (2, 4, 8, 16)
N_GROUPS_C = len(POOL_WINDOWS)
G_C = D_C // N_GROUPS_C
POOL_BUF = max(POOL_WINDOWS) - 1
D_FF = 4 * D_MODEL
N_BRANCH = 3
ALPHA = float((2 * DEPTH) ** 0.25)
BETA = float((8 * DEPTH) ** -0.25)
LN_EPS = 1e-5
IN_SPLITS = (D_A, 2 * D_A, 2 * D_A + D_B, 2 * D_A + 2 * D_B, 2 * D_A + 3 * D_B, 2 * D_A + 3 * D_B + D_C)
D_IN = 2 * D_A + 3 * D_B + D_C + N_BRANCH * D_MODEL

kernel_name = "hybrid_gmlp_conv_pool_decoder_step"


def layer_norm(x, g, b):
    xf = x.astype(jnp.float32)
    mu = jnp.mean(xf, axis=-1, keepdims=True)
    var = jnp.mean(jnp.square(xf - mu), axis=-1, keepdims=True)
    y = (xf - mu) * lax.rsqrt(var + LN_EPS) * g.astype(jnp.float32) + b.astype(jnp.float32)
    return y.astype(x.dtype)


def chunk_mixer(u, v, lnv_g, lnv_b, w_s, b_s):
    bn, t, _ = u.shape
    vn = layer_norm(v, lnv_g, lnv_b)
    n_chunks = -(-t // CHUNK)
    pad = n_chunks * CHUNK - t
    vp = jnp.pad(vn, ((0, 0), (0, pad), (0, 0)))
    vc = vp.reshape(bn, n_chunks, CHUNK, N_GROUPS_A, D_A // N_GROUPS_A)
    mask = jnp.tril(jnp.ones((CHUNK, CHUNK), dtype=bool))
    ws = jnp.where(mask[None], w_s, 0.0).astype(v.dtype)
    s = jnp.einsum('gts,bnsgc->bntgc', ws, vc) + b_s.T[None, None, :, :, None]
    s = s.reshape(bn, n_chunks * CHUNK, D_A)[:, :t]
    return u * s, vn


def conv_mixer(bg, cg, xb, buf, conv_w, conv_b):
    t = xb.shape[1]
    z = cg * xb
    zp = jnp.concatenate([buf.astype(z.dtype), z], axis=1)
    y = conv_b + sum(conv_w[k] * zp[:, k:k + t] for k in range(CONV_W))
    return bg * y, zp[:, -(CONV_W - 1):]


def pool_mixer(p, buf, pos0, w_pool, pool_scale):
    bn, t, _ = p.shape
    pp = jnp.concatenate([buf.astype(p.dtype), p], axis=1)
    cs = jnp.pad(jnp.cumsum(pp.astype(jnp.float32), axis=1), ((0, 0), (1, 0), (0, 0)))
    hi = cs[:, POOL_BUF + 1:POOL_BUF + 1 + t]
    pos = pos0 + jnp.arange(t)
    means = []
    for g, w in enumerate(POOL_WINDOWS):
        sl = slice(g * G_C, (g + 1) * G_C)
        lo = cs[:, POOL_BUF + 1 - w:POOL_BUF + 1 - w + t, sl]
        cnt = jnp.minimum(pos + 1, w).astype(jnp.float32)[None, :, None]
        means.append((hi[..., sl] - lo) / cnt)
    mean = jnp.concatenate(means, axis=-1).astype(p.dtype)
    d = (mean - p).reshape(bn, t, N_GROUPS_C, G_C)
    y = jnp.einsum('btgc,gcd->btgd', d, w_pool).reshape(bn, t, D_C) * pool_scale
    return y, pp[:, -POOL_BUF:]


def trunk_layer(x, buf_conv, buf_pool, pos0, w_in, lnv_g, lnv_b, w_spatial, b_spatial, w_proj_a,
                conv_w, conv_b, w_proj_b, w_pool, pool_scale, w_proj_c, w_o,
                ln1_g, ln1_b, w_ff1, w_ff2, ln2_g, ln2_b):
    bn, t, _ = x.shape
    proj = x @ w_in
    u, v, bg, cg, xb, xc, gates = jnp.split(proj, IN_SPLITS, axis=-1)
    ha, vn = chunk_mixer(u, v, lnv_g, lnv_b, w_spatial, b_spatial)
    hb, new_conv = conv_mixer(bg, cg, xb, buf_conv, conv_w, conv_b)
    hc, new_pool = pool_mixer(xc, buf_pool, pos0, w_pool, pool_scale)
    gt = jax.nn.sigmoid(gates.astype(jnp.float32)).astype(x.dtype).reshape(bn, t, N_BRANCH, D_MODEL)
    merged = gt[:, :, 0] * (ha @ w_proj_a) + gt[:, :, 1] * (hb @ w_proj_b) + gt[:, :, 2] * (hc @ w_proj_c)
    h = layer_norm(ALPHA * x + merged @ w_o, ln1_g, ln1_b)
    f = jnp.square(jax.nn.relu(h @ w_ff1)) @ w_ff2
    out = layer_norm(ALPHA * h + f, ln2_g, ln2_b)
    return out, new_conv, new_pool, vn


def setup_inputs(seed: int = 0) -> dict:
    key = jax.random.key(seed)
    ks = jax.random.split(key, 24)
    f32 = jnp.float32
    nrm = lambda k, shape, s: (jax.random.normal(k, shape, f32) * s).astype(f32)
    return {
        'x_prompt': nrm(ks[0], (BATCH, SEQ, D_MODEL), 1.0),
        'x_sample': nrm(ks[1], (DEC_BATCH, DEC_SEQ, D_MODEL), 1.0),
        'state_conv': nrm(ks[2], (DEPTH, DEC_BATCH, CONV_W - 1, D_B), 1.0),
        'state_pool': nrm(ks[3], (DEPTH, DEC_BATCH, POOL_BUF, D_C), 1.0),
        'w_in': nrm(ks[4], (DEPTH, D_MODEL, D_IN), D_MODEL ** -0.5),
        'lnv_g': 1.0 + nrm(ks[5], (DEPTH, D_A), 0.02),
        'lnv_b': nrm(ks[6], (DEPTH, D_A), 0.02),
        'w_spatial': nrm(ks[7], (DEPTH, N_GROUPS_A, CHUNK, CHUNK), CHUNK ** -0.5),
        'b_spatial': 1.0 + nrm(ks[8], (DEPTH, N_GROUPS_A, CHUNK), 0.02),
        'w_proj_a': nrm(ks[9], (DEPTH, D_A, D_MODEL), BETA * D_A ** -0.5),
        'conv_w': nrm(ks[10], (DEPTH, CONV_W, D_B), CONV_W ** -0.5),
        'conv_b': nrm(ks[11], (DEPTH, D_B), 0.02),
        'w_proj_b': nrm(ks[12], (DEPTH, D_B, D_MODEL), BETA * D_B ** -0.5),
        'w_pool': nrm(ks[13], (DEPTH, N_GROUPS_C, G_C, G_C), G_C ** -0.5),
        'pool_scale': 0.5 + nrm(ks[14], (DEPTH, D_C), 0.1),
        'w_proj_c': nrm(ks[15], (DEPTH, D_C, D_MODEL), BETA * D_C ** -0.5),
        'w_o': nrm(ks[16], (DEPTH, D_MODEL, D_MODEL), BETA * D_MODEL ** -0.5),
        'ln1_g': 1.0 + nrm(ks[17], (DEPTH, D_MODEL), 0.02),
        'ln1_b': nrm(ks[18], (DEPTH, D_MODEL), 0.02),
        'w_ff1': nrm(ks[19], (DEPTH, D_MODEL, D_FF), D_MODEL ** -0.5),
        'w_ff2': nrm(ks[20], (DEPTH, D_FF, D_MODEL), BETA * D_FF ** -0.5),
        'ln2_g': 1.0 + nrm(ks[21], (DEPTH, D_MODEL), 0.02),
        'ln2_b': nrm(ks[22], (DEPTH, D_MODEL), 0.02),
    }


def reference(x_prompt, x_sample, state_conv, state_pool, w_in, lnv_g, lnv_b, w_spatial, b_spatial,
              w_proj_a, conv_w, conv_b, w_proj_b, w_pool, pool_scale, w_proj_c, w_o,
              ln1_g, ln1_b, w_ff1, w_ff2, ln2_g, ln2_b):
    bp = x_prompt.shape[0]
    xp = x_prompt
    xs = x_sample
    conv_p, pool_p, conv_s, pool_s, chunk_v_s = [], [], [], [], []
    for l in range(DEPTH):
        params = (w_in[l], lnv_g[l], lnv_b[l], w_spatial[l], b_spatial[l], w_proj_a[l],
                  conv_w[l], conv_b[l], w_proj_b[l], w_pool[l], pool_scale[l], w_proj_c[l], w_o[l],
                  ln1_g[l], ln1_b[l], w_ff1[l], w_ff2[l], ln2_g[l], ln2_b[l])
        zero_conv = jnp.zeros((bp, CONV_W - 1, D_B), xp.dtype)
        zero_pool = jnp.zeros((bp, POOL_BUF, D_C), xp.dtype)
        xp, nc_p, np_p, _ = trunk_layer(xp, zero_conv, zero_pool, 0, *params)
        xs, nc_s, np_s, vn_s = trunk_layer(xs, state_conv[l], state_pool[l], PAST_LEN, *params)
        conv_p.append(nc_p)
        pool_p.append(np_p)
        conv_s.append(nc_s)
        pool_s.append(np_s)
        chunk_v_s.append(vn_s)
    new_conv_prompt = jnp.stack(conv_p)
    new_pool_prompt = jnp.stack(pool_p)
    new_conv_sample = jnp.stack(conv_s)
    new_pool_sample = jnp.stack(pool_s)
    chunk_v_sample = jnp.stack(chunk_v_s)
    return (xp, xs, new_conv_prompt, new_pool_prompt, new_conv_sample, new_pool_sample, chunk_v_sample)
```

```python
import numpy as np
from contextlib import ExitStack
import concourse.bass as bass
import concourse.mybir as mybir
from concourse.bass_utils import run_bass_kernel_spmd

F32 = mybir.dt.float32
BF16 = mybir.dt.bfloat16
AF = mybir.ActivationFunctionType
ALU = mybir.AluOpType

D = 1024
DEPTH = 2
NCORE = 8
SEQ = 2048
NSAMP = 16
NPASS = 2
PT = SEQ // NPASS
SP_ = NSAMP
NTOK = PT + 128
NT128 = PT // 128
WT = [(0, 512), (512, 512), (PT, SP_)]
ALPHA = float((2 * DEPTH) ** 0.25)
LN_EPS = 1e-5
POOL_W = (2, 4, 8, 16)
NSLOT = 10
STOP_AFTER = None


class _Stop(Exception):
    pass


def _ck(k):
    if STOP_AFTER is not None and k and k >= STOP_AFTER:
        raise _Stop()
C_U, C_V, C_BG, C_CG, C_XB, C_XC, C_GATE = 0, 1024, 2048, 3072, 4096, 5120, 6144
PV_LNVG, PV_LNVB, PV_CW0, PV_CW1, PV_CW2, PV_CB, PV_PS = range(7)
NPV = 7


class Buf:
    __slots__ = ("name", "lw", "rd")

    def __init__(self, name, init=None):
        self.name = name
        self.lw = dict(init) if init else {}
        self.rd = {}


def _merge(dst, src):
    for k, v in src.items():
        if dst.get(k, -1) < v:
            dst[k] = v


class Op:
    __slots__ = ("emit", "raw", "other", "tok", "dma")


class Sched:
    COMPUTE = ("pe", "act", "dve", "pool")
    NDMASEM = 10

    def __init__(self):
        self.ops = {e: [] for e in ("pe", "act", "dve", "pool", "sp")}
        self.cnt = {e: 0 for e in self.COMPUTE}
        self.dma_n = {"sp": 0, "pool": 0}
        self.dma_cnt = {}

    def add(self, eng, emit, reads=(), writes=(), dma=False):
        op = Op()
        op.emit = emit
        op.dma = dma
        raw, other = {}, {}
        for b in reads:
            _merge(raw, b.lw)
        for b in writes:
            _merge(other, b.lw)
            _merge(other, b.rd)
        if dma:
            i = self.dma_n[eng] % self.NDMASEM
            self.dma_n[eng] += 1
            key = ("dma", eng, i)
            n = self.dma_cnt.get(key, 0)
            if n > 0:
                _merge(other, {key: 16 * n})
            self.dma_cnt[key] = n + 1
            tok = (key, 16 * (n + 1))
        else:
            self.cnt[eng] += 1
            tok = (eng, self.cnt[eng])
        op.raw, op.other, op.tok = raw, other, tok
        t = {tok[0]: tok[1]}
        for b in writes:
            b.lw = dict(t)
            b.rd = {}
        for b in reads:
            _merge(b.rd, t)
        self.ops[eng].append(op)
        return tok

    @staticmethod
    def retag(bufs):
        d = {}
        for b in bufs:
            _merge(d, b.lw)
            _merge(d, b.rd)
        return d

    def emit_all(self, nc, block, sems, final_waits):
        sched = self

        def run(engname, e):
            known = {}
            for op in sched.ops[engname]:
                deps = {}
                for k, v in op.raw.items():
                    if k == engname and engname == "pe":
                        continue
                    deps[k] = max(deps.get(k, -1), v)
                for k, v in op.other.items():
                    if k == engname:
                        continue
                    deps[k] = max(deps.get(k, -1), v)
                for k, v in deps.items():
                    if known.get(k, -1) >= v:
                        continue
                    e.wait_ge(sems[k], v)
                    known[k] = v
                ins = op.emit(e)
                if op.dma:
                    ins.then_inc(sems[op.tok[0]], 16)
                else:
                    ins.then_inc(sems[op.tok[0]], 1)
            if engname == "sp":
                for k, v in final_waits.items():
                    if known.get(k, -1) < v:
                        e.wait_ge(sems[k], v)

        @block.tensor
        def _(e):
            run("pe", e)

        @block.scalar
        def _(e):
            run("act", e)

        @block.vector
        def _(e):
            run("dve", e)

        @block.gpsimd
        def _(e):
            run("pool", e)

        @block.sync
        def _(e):
            run("sp", e)


def _host_consts():
    ident = np.eye(128, dtype=np.float32)
    s = np.arange(128)[:, None]
    t = np.arange(128)[None, :]
    mask = (s <= t).astype(np.float32)
    bands = np.zeros((128, 12, 128), np.float32)
    for g, w in enumerate(POOL_W):
        main = ((s <= t) & (s > t - w)).astype(np.float32) / w - (s == t).astype(np.float32)
        cnt = np.minimum(t + 1, w).astype(np.float32)
        first = ((s <= t) & (s > t - w)).astype(np.float32) / cnt - (s == t).astype(np.float32)
        halo = ((s - 128) > (t - w)).astype(np.float32) / w
        bands[:, g * 3 + 0, :] = main
        bands[:, g * 3 + 1, :] = first
        bands[:, g * 3 + 2, :] = halo
    sel = np.zeros((128, 2, 4, SP_), np.float32)
    i8w = np.zeros((128, 4, SP_), np.float32)
    for g, w in enumerate(POOL_W):
        for b in range(SP_):
            for r in range(15):
                if r >= 15 - (w - 1):
                    sel[(b % 8) * 15 + r, b // 8, g, b] = 1.0 / w
            i8w[b, g, b] = 1.0 / w - 1.0
    return ident, mask, bands, sel, i8w


def build_nc():
    nc = bass.Bass("TRN2", target_bir_lowering=False)
    es = ExitStack()

    def din(name, shape):
        return nc.dram_tensor(name, list(shape), F32, kind="ExternalInput").ap()

    def dout(name, shape):
        return nc.dram_tensor(name, list(shape), F32, kind="ExternalOutput").ap()

    x_p = din("x_p", [SEQ, D])
    x_s = din("x_s", [NSAMP, D])
    sconv = din("sconv", [DEPTH, NSAMP, 2, D])
    spool = din("spool", [DEPTH, NSAMP, 15, D])
    w_in = din("w_in", [DEPTH, D, 9216])
    w_pa = din("w_proj_a", [DEPTH, D, D])
    w_pb = din("w_proj_b", [DEPTH, D, D])
    w_pc = din("w_proj_c", [DEPTH, D, D])
    w_o = din("w_o", [DEPTH, D, D])
    w_f1 = din("w_ff1", [DEPTH, D, 4 * D])
    w_f2 = din("w_ff2", [DEPTH, 4 * D, D])
    w_pool = din("w_pool", [DEPTH, 4, 256, 256])
    pvec_d = din("pvec", [128, DEPTH * NPV * 8])
    lnbc_d = din("lnbc", [DEPTH, 4, 128, D])
    wsT_d = din("wsT", [DEPTH, 128, 4 * 128])
    bsbc_d = din("bsbc", [DEPTH, 128, 4 * 128])
    ws00_d = din("ws00", [DEPTH, 128, 4])
    bs0_d = din("bs0", [DEPTH, 128, 4])
    c_ident = din("c_ident", [128, 128])
    c_mask = din("c_mask", [128, 128])
    c_bands = din("c_bands", [128, 12 * 128])
    c_sel = din("c_sel", [128, 2 * 4 * SP_])
    c_i8w = din("c_i8w", [128, 4 * SP_])

    y_p = dout("y_p", [SEQ, D])
    y_s = dout("y_s", [NSAMP, D])
    ncp = dout("ncp", [DEPTH, 2, D])
    npp = dout("npp", [DEPTH, 15, D])
    ncs = dout("ncs", [DEPTH, NSAMP, 2, D])
    nps = dout("nps", [DEPTH, NSAMP, 15, D])
    cvs = dout("cvs", [DEPTH, NSAMP, D])

    def sb(name, shape, dt=F32):
        return es.enter_context(nc.sbuf_tensor(name, list(shape), dt))

    R = sb("R", [128, 9, D])
    T = sb("T", [128, 8, NTOK], BF16)
    B1 = sb("B1", [128, 9 * D], BF16)
    B2 = sb("B2", [128, 8 * NTOK], BF16)
    ring = sb("ring", [128, NSLOT, 8, 256], BF16)
    lnbc = sb("lnbc_sb", [128, 4, D])
    Dt = sb("Dt", [128, 2, 8, 128], BF16)
    NFT = 6
    ftmp = sb("ftmp", [128, NFT, 512])
    zbuf = sb("zbuf", [128, 3, 514], BF16)
    zsb = sb("zsb", [128, 8, SP_, 3], BF16)
    xctm = sb("xctm", [128, 3, D], BF16)
    xccar = sb("xccar", [128, DEPTH, D], BF16)
    zcar = sb("zcar", [128, DEPTH, 8, 2], BF16)
    stage = sb("stage", [128, 2, D])
    ident = sb("ident", [128, 128])
    ones = sb("ones", [128, 128])
    maskt = sb("maskt", [128, 128])
    pvec = sb("pvec_sb", [128, DEPTH * NPV * 8])
    bands = sb("bands", [128, 12, 128], BF16)
    selb = sb("selb", [128, 2, 4, SP_], BF16)
    i8wb = sb("i8wb", [128, 4, SP_], BF16)
    WsT = sb("WsT", [128, 4, 128], BF16)
    WsS = sb("WsS", [128, 4, SP_], BF16)
    diag = sb("diag", [128, 8, 3, 128], BF16)
    Bias2 = sb("Bias2", [128, 8, 128])
    Bias2s = sb("Bias2s", [128, 8])
    ws00 = sb("ws00_sb", [128, 4])
    bs0 = sb("bs0_sb", [128, 4])
    zsf = sb("zsf", [128, 8, 128])
    vnT = zsf
    NSM = 8
    stat = sb("stat", [128, NSM, 12])
    mv = sb("mv", [128, NSM, 2])
    sm = sb("sm", [128, NSM, 4])
    epsb = sb("epsb", [128, 1])

    ps = es.enter_context(nc.psum_tensor("ps", [128, 8 * 512], F32))

    S = Sched()

    bank = [Buf(f"bank{i}") for i in range(8)]
    slotb = [Buf(f"slot{i}") for i in range(NSLOT)]
    Rb = [Buf(f"R{i}") for i in range(9)]
    Tb = [Buf(f"T{i}") for i in range(9)]
    ftb = [Buf(f"ft{i}") for i in range(NFT)]
    zbb = [Buf("zb0"), Buf("zb1"), Buf("zb2")]
    zsbb = Buf("zsb")
    xcb = [Buf("xc0"), Buf("xc1"), Buf("xc2")]
    xccb = [Buf("xcc0"), Buf("xcc1")]
    zcb = [Buf("zc0"), Buf("zc1")]
    ppb = Buf("pp")
    stg = [Buf("stg0"), Buf("stg1")]
    Dtb = [Buf("Dt0"), Buf("Dt1")]
    lnb = Buf("lnbc")
    cst = Buf("consts")
    cst2 = Buf("consts_pool")
    cst3 = Buf("consts_dve")
    lcst = Buf("layer_consts")
    smallb = [Buf(f"small{i}") for i in range(8)]
    zsfb = Buf("zsf")
    vnb = zsfb
    state = {"bank1": 0, "bank2": 0, "slot": 0, "ft": 0, "stg": 0, "small": 0, "zb": 0, "xc": 0, "dt": 0}

    def bank1():
        i = state["bank1"] % 8
        state["bank1"] += 1
        return i

    def bank2():
        i = state["bank2"] % 4
        state["bank2"] += 1
        return 2 * i

    def pcols(b, a, n):
        return ps[:, b * 512 + a: b * 512 + a + n]

    def ft_next():
        i = state["ft"] % NFT
        state["ft"] += 1
        return i

    def stg_next():
        i = state["stg"] % 2
        state["stg"] += 1
        return i

    def small_next():
        i = state["small"] % 8
        state["small"] += 1
        return i

    def dma_sp(out, in_, reads=(), writes=()):
        return S.add("sp", lambda e: e.dma_start(out=out, in_=in_), reads, writes, dma=True)

    def dma_pool(out, in_, reads=(), writes=()):
        return S.add("pool", lambda e: e.dma_start(out=out, in_=in_), reads, writes, dma=True)

    def load_unit(src2d):
        s = state["slot"] % NSLOT
        state["slot"] += 1
        dma_pool(ring[:, s, :, :], src2d.rearrange("(kc p) n -> p kc n", p=128), writes=[slotb[s]])
        return s

    def load_dunit(src2d):
        if state["slot"] % 2 == 1:
            state["slot"] += 1
        s = state["slot"] % NSLOT
        state["slot"] += 2
        view = ring[:, s:s + 2, :, :].rearrange("p a k n -> p (a k n)").rearrange("p (k n) -> p k n", k=8)
        dma_pool(view, src2d.rearrange("(kc p) n -> p kc n", p=128), writes=[slotb[s], slotb[s + 1]])
        return (s, view)

    def pe_group(mms, reads, writes):
        def emit(e):
            ins = None
            for (o, l, r, st, sp_) in mms:
                ins = e.matmul(o, l, r, start=st, stop=sp_)
            return ins
        return S.add("pe", emit, reads, writes)

    def pe_transposes(items, reads, writes):
        def emit(e):
            ins = None
            for (o, i, idn) in items:
                ins = e.transpose(o, i, idn)
            return ins
        return S.add("pe", emit, reads, writes)

    def act(out, in_, func, reads, writes, bias=None, scale=None, accum_out=None):
        kw = {}
        if bias is not None:
            kw["bias"] = bias
        if scale is not None:
            kw["scale"] = scale
        if accum_out is not None:
            kw["accum_out"] = accum_out
        return S.add("act", lambda e: e.activation(out=out, in_=in_, func=func, **kw), reads, writes)

    def dve_tt(out, in0, in1, op, reads, writes):
        return S.add("dve", lambda e: e.tensor_tensor(out=out, in0=in0, in1=in1, op=op), reads, writes)

    def dve_ts(out, in0, s1, s2, op0, op1, reads, writes):
        if op1 is None:
            return S.add("dve", lambda e: e.tensor_scalar(out=out, in0=in0, scalar1=s1, scalar2=None, op0=op0),
                         reads, writes)
        return S.add("dve", lambda e: e.tensor_scalar(out=out, in0=in0, scalar1=s1, scalar2=s2, op0=op0, op1=op1),
                     reads, writes)

    def dve_stt(out, in0, scalar, in1, op0, op1, reads, writes, accum_out=None):
        if accum_out is not None:
            return S.add("dve", lambda e: e.scalar_tensor_tensor(out=out, in0=in0, scalar=scalar, in1=in1,
                                                                 op0=op0, op1=op1, accum_out=accum_out),
                         reads, writes)
        return S.add("dve", lambda e: e.scalar_tensor_tensor(out=out, in0=in0, scalar=scalar, in1=in1,
                                                             op0=op0, op1=op1), reads, writes)

    def dve_copy(out, in_, reads, writes):
        return S.add("dve", lambda e: e.tensor_copy(out=out, in_=in_), reads, writes)

    def pv(l, v, c):
        i = (l * NPV + v) * 8 + c
        return pvec[:, i:i + 1]

    dma_sp(ident[:, :], c_ident, writes=[cst])
    dma_sp(maskt[:, :], c_mask, writes=[cst])
    dma_sp(pvec[:, :], pvec_d, writes=[cst])
    def load_pool_consts():
        dma_pool(bands[:, :, :], c_bands.rearrange("p (a b) -> p a b", a=12), writes=[cst2])
        dma_pool(selb[:, :, :, :], c_sel.rearrange("p (h a b) -> p h a b", h=2, a=4), writes=[cst2])
        dma_pool(i8wb[:, :, :], c_i8w.rearrange("p (a b) -> p a b", a=4), writes=[cst2])
    S.add("dve", lambda e: e.memset(ones[:, :], 1.0), (), [cst3])
    S.add("dve", lambda e: e.memset(epsb[:, :], LN_EPS), (), [cst3])
    S.add("dve", lambda e: e.memset(zsf[:, :, :], 0.0), (), [zsfb])

    out_tokens = {}

    def passthrough_copies():
        for l_ in range(DEPTH):
            tk = dma_sp(ncs[l_, :, 0, :], sconv[l_, :, 1, :])
            out_tokens[tk[0]] = max(out_tokens.get(tk[0], 0), tk[1])
            for h in range(2):
                tk = dma_sp(nps[l_, h * 8:(h + 1) * 8, 0:14, :], spool[l_, h * 8:(h + 1) * 8, 1:15, :])
                out_tokens[tk[0]] = max(out_tokens.get(tk[0], 0), tk[1])

    def out_dma(out, in_, reads):
        tk = dma_sp(out, in_, reads=reads)
        out_tokens[tk[0]] = max(out_tokens.get(tk[0], 0), tk[1])

    def tile_rows(i):
        return 128

    def tile_cols(i):
        return (i * 128, 128) if i < NT128 else (PT, SP_)

    def wt_tiles(w):
        return list(range(4 * w, 4 * w + 4)) if w < 2 else [8]

    def transpose_R_to_T(i):
        rows = tile_rows(i)
        c0, n = tile_cols(i)
        n = 128
        b = bank2()
        items = []
        for c in range(8):
            items.append((ps[:, b * 512 + c * 128: b * 512 + c * 128 + rows],
                          R[0:rows, i, c * 128:(c + 1) * 128], ident[0:rows, 0:rows]))
        pe_transposes(items, [Rb[i], cst], [bank[b], bank[b + 1]])
        src = ps[:, b * 512:(b + 2) * 512].rearrange("p (c t) -> p c t", c=8)[:, :, 0:rows]
        act(T[:, :, c0:c0 + n], src, AF.Copy, [bank[b], bank[b + 1]], [Tb[i]])

    def xstat_project(i, slots, dst_reads, lhs_of_kc):
        rows = tile_rows(i)
        b = bank2()
        mms = []
        for h in range(2):
            for kc in range(8):
                mms.append((ps[0:rows, (b + h) * 512:(b + h + 1) * 512],
                            lhs_of_kc(kc), slots[h][1][:, kc, :], kc == 0, kc == 7))
        sl = []
        for (s0, _v) in slots:
            sl += [slotb[s0], slotb[s0 + 1]]
        pe_group(mms, dst_reads + sl, [bank[b], bank[b + 1]])
        return b

    def ln_stats(src_ap_fn, rows, reads):
        k = small_next()
        sbuf_ = smallb[k]
        a0, a1 = src_ap_fn(0), src_ap_fn(1)
        S.add("dve", lambda e: e.bn_stats(out=stat[0:rows, k, 0:6], in_=a0), reads, [sbuf_])
        S.add("dve", lambda e: e.bn_stats(out=stat[0:rows, k, 6:12], in_=a1), reads, [sbuf_])
        S.add("dve", lambda e: e.bn_aggr(out=mv[0:rows, k, :], in_=stat[0:rows, k, :]), [sbuf_], [sbuf_])
        act(sm[0:rows, k, 0:1], mv[0:rows, k, 1:2], AF.Sqrt, [sbuf_, cst], [sbuf_], bias=epsb[0:rows, :], scale=1.0)
        S.add("dve", lambda e: e.reciprocal(out=sm[0:rows, k, 1:2], in_=sm[0:rows, k, 0:1]), [sbuf_], [sbuf_])
        dve_stt(sm[0:rows, k, 2:3], mv[0:rows, k, 0:1], -1.0, sm[0:rows, k, 1:2], ALU.mult, ALU.mult,
                [sbuf_], [sbuf_])
        return sm[0:rows, k, 1:2], sm[0:rows, k, 2:3], sbuf_

    def run_pipeline(n_items, stages, lags):
        for t in range(n_items + max(lags)):
            for f, lag in zip(stages, lags):
                i = t - lag
                if 0 <= i < n_items:
                    f(i)

    def ln_pipeline(pre, gi, post):
        st = {}

        def s_stats(i):
            k, sbuf_ = st[i]
            dve_ts(mv[:, k, 0:1], stat[:, k, 0:1], 1.0 / D, None, ALU.mult, None, [sbuf_], [sbuf_])
            dve_tt(stat[:, k, 2:3], mv[:, k, 0:1], mv[:, k, 0:1], ALU.mult, [sbuf_], [sbuf_])
            dve_stt(mv[:, k, 1:2], stat[:, k, 1:2], 1.0 / D, stat[:, k, 2:3], ALU.mult, ALU.subtract,
                    [sbuf_], [sbuf_])
            act(sm[:, k, 0:1], mv[:, k, 1:2], AF.Sqrt, [sbuf_, cst3], [sbuf_], bias=epsb[:, :], scale=1.0)

        def s_rstd(i):
            k, sbuf_ = st[i]
            S.add("dve", lambda e: e.reciprocal(out=sm[:, k, 1:2], in_=sm[:, k, 0:1]), [sbuf_], [sbuf_])
            dve_stt(sm[:, k, 2:3], mv[:, k, 0:1], -1.0, sm[:, k, 1:2], ALU.mult, ALU.mult, [sbuf_], [sbuf_])
            act(R[:, i, :], R[:, i, :], AF.Identity, [Rb[i], sbuf_], [Rb[i]], bias=sm[:, k, 2:3], scale=sm[:, k, 1:2])

        def s_affine(i):
            dve_tt(R[:, i, :], R[:, i, :], lnbc[:, gi, :], ALU.mult, [Rb[i], lnb], [Rb[i]])
            dve_tt(R[:, i, :], R[:, i, :], lnbc[:, gi + 1, :], ALU.add, [Rb[i], lnb], [Rb[i]])

        def s_pre(i):
            k = small_next()
            sbuf_ = smallb[k]
            st[i] = (k, sbuf_)
            pre(i, stat[:, k, 0:1], sbuf_)
            sj = stg_next()
            act(stage[:, sj, :], R[:, i, :], AF.Square, [Rb[i]], [stg[sj], sbuf_], accum_out=stat[:, k, 1:2])

        run_pipeline(cur["NTI"], [s_pre, s_affine, s_stats, s_rstd, post], [0, 2, 0, 1, 3])

    def merge_branch(l, br, hbufs, wproj, first, Mb_new):
        hv = B2[:, :].rearrange("p (c t) -> p c t", c=8)
        mg = B1[:, 0:8 * NTOK].rearrange("p (c t) -> p c t", c=8)
        items = []
        units = {}

        def g_stage(k):
            jp, jj, w, t0, n = items[k]
            if jp not in units:
                units[jp] = (load_unit(wproj[l, :, jp * 256:(jp + 1) * 256]),
                             load_unit(w_in[l, :, C_GATE + br * D + jp * 256: C_GATE + br * D + (jp + 1) * 256]))
            sg = units[jp][1]
            bg_ = bank1()
            pe_group([(pcols(bg_, 0, n), ring[:, sg, kc, jj * 128:(jj + 1) * 128], T[:, kc, t0:t0 + n],
                       kc == 0, kc == 7) for kc in range(8)],
                     [slotb[sg]] + [Tb[i] for i in wt_tiles(w)], [bank[bg_]])
            f = ft_next()
            act(ftmp[:, f, 0:n], pcols(bg_, 0, n), AF.Sigmoid, [bank[bg_]], [ftb[f]])
            items[k] = (jp, jj, w, t0, n, f)

        def p_stage(k):
            jp, jj, w, t0, n, f = items[k]
            j = jp * 2 + jj
            sp_ = units[jp][0]
            bp = bank1()
            pe_group([(pcols(bp, 0, n), ring[:, sp_, kc, jj * 128:(jj + 1) * 128], hv[:, kc, t0:t0 + n],
                       kc == 0, kc == 7) for kc in range(8)],
                     [slotb[sp_]] + [hbufs[(kc, w)] for kc in range(8)], [bank[bp]])
            mb = Mb_new[(j, w)]
            if first:
                dve_tt(mg[:, j, t0:t0 + n], ftmp[:, f, 0:n], pcols(bp, 0, n), ALU.mult,
                       [ftb[f], bank[bp]], [mb])
            else:
                f2 = ft_next()
                dve_tt(ftmp[:, f2, 0:n], ftmp[:, f, 0:n], pcols(bp, 0, n), ALU.mult,
                       [ftb[f], bank[bp]], [ftb[f2]])
                dve_tt(mg[:, j, t0:t0 + n], mg[:, j, t0:t0 + n], ftmp[:, f2, 0:n], ALU.add,
                       [ftb[f2], mb], [mb])

        for jp in range(4):
            for jj in range(2):
                for w, (t0, n) in enumerate(cur["WT"]):
                    items.append((jp, jj, w, t0, n))
        run_pipeline(len(items), [g_stage, p_stage], [0, 1])

    def build_layer_consts(l2, has_s2):
        si = stg_next()
        dma_sp(stage[:, si, 0:512], wsT_d[l2, :, :], writes=[stg[si]])
        si2 = stg_next()
        dma_sp(stage[:, si2, 0:512], bsbc_d[l2, :, :], writes=[stg[si2]])
        dma_sp(ws00[:, :], ws00_d[l2, :, :], writes=[lcst])
        dma_sp(bs0[:, :], bs0_d[l2, :, :], writes=[lcst])
        st3 = stage[:, si, 0:512].rearrange("p (g t) -> p g t", g=4)
        for g in range(4):
            dve_tt(st3[:, g, :], st3[:, g, :], maskt[:, :], ALU.mult, [stg[si], cst], [stg[si]])
        dve_copy(WsT[:, :, :], st3, [stg[si]], [lcst])
        br_ = bank1()
        pe_group([(pcols(br_, 0, 512), ones[:, :], stage[:, si, 0:512], True, True)],
                 [stg[si], cst3], [bank[br_]])
        bs3 = stage[:, si2, 0:512].rearrange("p (g t) -> p g t", g=4)
        for c in range(8):
            g = c // 2
            dve_stt(Bias2[:, c, :], pcols(br_, g * 128, 128), pv(l2, PV_LNVB, c), bs3[:, g, :],
                    ALU.mult, ALU.add, [bank[br_], stg[si2], cst], [lcst])
            dve_stt(Bias2s[:, c:c + 1], ws00[:, g:g + 1], pv(l2, PV_LNVB, c), bs0[:, g:g + 1],
                    ALU.mult, ALU.add, [lcst, cst], [lcst])
            for k3 in range(3):
                dve_ts(diag[:, c, k3, :], ident[:, :], pv(l2, PV_CW0 + k3, c), None, ALU.mult, None,
                       [cst], [lcst])
        for g in range(4):
            dve_ts(WsS[:, g, :], ident[:, 0:SP_], ws00[:, g:g + 1], None, ALU.mult, None,
                   [cst, lcst], [lcst])
        if has_s2:
            si = stg_next()
            dma_sp(stage[0:2 * SP_, si, :],
                   sconv[l2, 0:SP_, :, :].rearrange("b k d -> (b k) d"), writes=[stg[si]])
            bz = bank2()
            pe_transposes([(pcols(bz, c * 128, 128), stage[:, si, c * 128:(c + 1) * 128], ident[:, :])
                           for c in range(8)], [stg[si], cst], [bank[bz], bank[bz + 1]])
            for c in range(8):
                dve_copy(zsb[:, c, :, 0:2], pcols(bz, c * 128, 2 * SP_).rearrange("p (b k) -> p b k", k=2),
                         [bank[bz], bank[bz + 1]], [zsbb])

    cur = {"NTI": 9, "WT": WT}
    B1all = [Buf("B1init")]
    B2all = [Buf("B2init")]

    try:
      _ck(0)
      for p in range(NPASS):
          tok_base = p * PT
          smp_base = 0
          has_s = (p == 0)
          NTI = 9 if has_s else 8
          WTp = WT if has_s else WT[:2]
          cur["NTI"], cur["WT"] = NTI, WTp
          if p == 0:
              for i in range(NT128):
                  dma_sp(R[:, i, :], x_p[tok_base + i * 128: tok_base + (i + 1) * 128, :], writes=[Rb[i]])
          if has_s:
              S.add("dve", lambda e: e.memset(R[:, 8, :], 0.0), (), [Rb[8]])
              dma_sp(R[0:SP_, 8, :], x_s[smp_base:smp_base + SP_, :], writes=[Rb[8]])
          _ck(0.5)
          for i in range(NTI):
              transpose_R_to_T(i)
              _ck(0.6 + 0.01 * i)

          _ck(1)
          for l in range(DEPTH):
              kstep = p * DEPTH + l

              _ck(2)
              dA = Sched.retag(B1all)
              XH = [Buf(f"xh{i}", dA) for i in range(9)]
              B1all = XH
              xh = B1[:, :].rearrange("p (i d) -> p i d", i=9)
              slots = [load_dunit(w_in[l, :, C_V + h * 512: C_V + (h + 1) * 512]) for h in range(2)]
              _ck(2.1)
              stA = {}

              def a1_proj(i, slots=slots):
                  c0, n = tile_cols(i)
                  b = xstat_project(i, slots, [Tb[i]], lambda kc, c0=c0: T[:, kc, c0:c0 + 128])
                  k = small_next()
                  sbuf_ = smallb[k]
                  stA[i] = (b, k, sbuf_)
                  a0, a1 = ps[:, b * 512:(b + 1) * 512], ps[:, (b + 1) * 512:(b + 2) * 512]
                  S.add("dve", lambda e: e.bn_stats(out=stat[:, k, 0:6], in_=a0), [bank[b]], [sbuf_])
                  S.add("dve", lambda e: e.bn_stats(out=stat[:, k, 6:12], in_=a1), [bank[b + 1]], [sbuf_])
                  S.add("dve", lambda e: e.bn_aggr(out=mv[:, k, :], in_=stat[:, k, :]), [sbuf_], [sbuf_])
                  act(sm[:, k, 0:1], mv[:, k, 1:2], AF.Sqrt, [sbuf_, cst3], [sbuf_], bias=epsb[:, :], scale=1.0)

              def a1_norm(i):
                  b, k, sbuf_ = stA[i]
                  S.add("dve", lambda e: e.reciprocal(out=sm[:, k, 1:2], in_=sm[:, k, 0:1]), [sbuf_], [sbuf_])
                  dve_stt(sm[:, k, 2:3], mv[:, k, 0:1], -1.0, sm[:, k, 1:2], ALU.mult, ALU.mult, [sbuf_], [sbuf_])
                  act(xh[:, i, :], ps[:, b * 512:(b + 2) * 512], AF.Identity,
                      [bank[b], bank[b + 1], sbuf_], [XH[i]], bias=sm[:, k, 2:3], scale=sm[:, k, 1:2])
                  if i == 8:
                      sx = stg_next()
                      stA["xhs"] = sx
                      act(stage[:, sx, :], ps[:, b * 512:(b + 2) * 512], AF.Identity,
                          [bank[b], bank[b + 1], sbuf_], [stg[sx]], bias=sm[:, k, 2:3], scale=sm[:, k, 1:2])

              run_pipeline(NTI, [a1_proj, a1_norm], [0, 1])
              _ck(3)
              if has_s:
                  bt = bank2()
                  sx = stA["xhs"]
                  pe_transposes([(pcols(bt, c * 128, 128), stage[:, sx, c * 128:(c + 1) * 128], ident[:, :])
                                 for c in range(8)], [stg[sx], cst], [bank[bt], bank[bt + 1]])
                  for c in range(8):
                      act(vnT[:, c, 0:SP_], pcols(bt, c * 128, SP_), AF.Identity, [bank[bt], bank[bt + 1], cst], [vnb],
                          bias=pv(l, PV_LNVB, c), scale=pv(l, PV_LNVG, c))
                  bt2 = bank2()
                  pe_transposes([(ps[:, bt2 * 512 + c * 128: bt2 * 512 + (c + 1) * 128], vnT[:, c, :], ident[:, :])
                                 for c in range(8)], [vnb, cst], [bank[bt2], bank[bt2 + 1]])
                  si = stg_next()
                  act(stage[0:SP_, si, :], ps[0:SP_, bt2 * 512:(bt2 + 2) * 512], AF.Copy,
                      [bank[bt2], bank[bt2 + 1]], [stg[si]])
                  out_dma(cvs[l, smp_base:smp_base + SP_, :], stage[0:SP_, si, :], [stg[si]])

              if kstep == 0:
                  build_layer_consts(l, has_s)
              for v in range(4):
                  dma_sp(lnbc[:, v, :], lnbc_d[l, v, :, :], writes=[lnb])
              if kstep == 0:
                  passthrough_copies()
                  load_pool_consts()

              _ck(4)
              dB2 = Sched.retag(B2all)
              HA = {(c, w): Buf(f"ha{c}_{w}", dB2) for c in range(8) for w in range(3)}
              B2all = list(HA.values())
              hv = B2[:, :].rearrange("p (c t) -> p c t", c=8)
              for cp in range(4):
                  su = load_unit(w_in[l, :, C_U + cp * 256: C_U + (cp + 1) * 256])
                  for cc in range(2):
                      c = cp * 2 + cc
                      g = c // 2
                      for w, (t0, n) in enumerate(cur["WT"]):
                          bu = bank1()
                          pe_group([(pcols(bu, 0, n), ring[:, su, kc, cc * 128:(cc + 1) * 128], T[:, kc, t0:t0 + n],
                                     kc == 0, kc == 7) for kc in range(8)],
                                   [slotb[su]] + [Tb[i] for i in wt_tiles(w)], [bank[bu]])
                          bs_ = bank1()
                          f = ft_next()
                          if w < 2:
                              pe_group([(pcols(bs_, j * 128, 128), xh[:, 4 * w + j, c * 128:(c + 1) * 128],
                                         WsT[:, g, :], True, True) for j in range(4)],
                                       [XH[4 * w + j] for j in range(4)] + [lcst], [bank[bs_]])
                              dve_stt(ftmp[:, f, :].rearrange("p (j t) -> p j t", j=4),
                                      pcols(bs_, 0, 512).rearrange("p (j t) -> p j t", j=4),
                                      pv(l, PV_LNVG, c),
                                      Bias2[:, c:c + 1, :].to_broadcast([128, 4, 128]),
                                      ALU.mult, ALU.add, [bank[bs_], lcst, cst], [ftb[f]])
                          else:
                              pe_group([(pcols(bs_, 0, SP_), xh[:, 8, c * 128:(c + 1) * 128],
                                         WsS[:, g, :], True, True)], [XH[8], lcst], [bank[bs_]])
                              dve_stt(ftmp[:, f, 0:SP_], pcols(bs_, 0, SP_), pv(l, PV_LNVG, c),
                                      Bias2s[:, c:c + 1].to_broadcast([128, SP_]),
                                      ALU.mult, ALU.add, [bank[bs_], lcst, cst], [ftb[f]])
                          dve_tt(hv[:, c, t0:t0 + n], ftmp[:, f, 0:n], pcols(bu, 0, n), ALU.mult,
                                 [ftb[f], bank[bu]], [HA[(c, w)]])

              _ck(5)
              dM = Sched.retag(B1all)
              MG = {(j, w): Buf(f"mg{j}_{w}", dM) for j in range(8) for w in range(3)}
              B1all = list(MG.values())
              merge_branch(l, 0, HA, w_pa, True, MG)

              _ck(6)
              dB2 = Sched.retag(B2all)
              HB = {(c, w): Buf(f"hb{c}_{w}", dB2) for c in range(8) for w in range(3)}
              B2all = list(HB.values())
              pendB = []

              def b_conv(item):
                  (c, w, t0, n, zsrc, zbufs, by, fbg) = item
                  pe_group([(pcols(by, 0, n), diag[:, c, k3, :], zsrc(k3), k3 == 0, k3 == 2)
                            for k3 in range(3)], zbufs + [lcst], [bank[by]])
                  dve_stt(hv[:, c, t0:t0 + n], pcols(by, 0, n), pv(l, PV_CB, c), ftmp[:, fbg, 0:n],
                          ALU.add, ALU.mult, [bank[by], ftb[fbg], cst], [HB[(c, w)]])

              for cp in range(4):
                  sbg = load_unit(w_in[l, :, C_BG + cp * 256: C_BG + (cp + 1) * 256])
                  scg = load_unit(w_in[l, :, C_CG + cp * 256: C_CG + (cp + 1) * 256])
                  sxb = load_unit(w_in[l, :, C_XB + cp * 256: C_XB + (cp + 1) * 256])
                  for cc in range(2):
                      c = cp * 2 + cc
                      prev_z = None
                      for w, (t0, n) in enumerate(cur["WT"]):
                          trd = [Tb[i] for i in wt_tiles(w)]
                          bks = []
                          for su_ in (sbg, scg, sxb):
                              bb = bank1()
                              pe_group([(pcols(bb, 0, n), ring[:, su_, kc, cc * 128:(cc + 1) * 128],
                                         T[:, kc, t0:t0 + n], kc == 0, kc == 7) for kc in range(8)],
                                       [slotb[su_]] + trd, [bank[bb]])
                              bks.append(bb)
                          bbg, bcg, bxb = bks
                          fcg = ft_next()
                          act(ftmp[:, fcg, 0:n], pcols(bcg, 0, n), AF.Copy, [bank[bcg]], [ftb[fcg]])
                          fbg = ft_next()
                          act(ftmp[:, fbg, 0:n], pcols(bbg, 0, n), AF.Copy, [bank[bbg]], [ftb[fbg]])
                          by = bank1()
                          if w < 2:
                              zi = state["zb"] % 3
                              state["zb"] += 1
                              dve_tt(zbuf[:, zi, 2:2 + n], ftmp[:, fcg, 0:n], pcols(bxb, 0, n), ALU.mult,
                                     [ftb[fcg], bank[bxb]], [zbb[zi]])
                              if w == 0:
                                  if p == 0:
                                      S.add("dve", lambda e, zi=zi: e.memset(zbuf[:, zi, 0:2], 0.0), (), [zbb[zi]])
                                  else:
                                      dve_copy(zbuf[:, zi, 0:2], zcar[:, l, c, :], [zcb[l]], [zbb[zi]])
                              else:
                                  dve_copy(zbuf[:, zi, 0:2], zbuf[:, prev_z, 512:514], [zbb[prev_z]], [zbb[zi]])
                                  if p == 0:
                                      dve_copy(zcar[:, l, c, :], zbuf[:, zi, 512:514], [zbb[zi]], [zcb[l]])
                                  else:
                                      dve_tt(zsf[:, c, SP_:SP_ + 2], ftmp[:, fcg, n - 2:n], pcols(bxb, n - 2, 2),
                                             ALU.mult, [ftb[fcg], bank[bxb]], [zsfb])
                              prev_z = zi
                              item = (c, w, t0, n, (lambda k3, zi=zi, n=n: zbuf[:, zi, k3:k3 + n]), [zbb[zi]], by, fbg)
                          else:
                              dve_tt(zsb[:, c, :, 2], ftmp[:, fcg, 0:n], pcols(bxb, 0, n), ALU.mult,
                                     [ftb[fcg], bank[bxb]], [zsbb])
                              dve_tt(zsf[:, c, 0:SP_], ftmp[:, fcg, 0:n], pcols(bxb, 0, n), ALU.mult,
                                     [ftb[fcg], bank[bxb]], [zsfb])
                              item = (c, w, t0, n, (lambda k3, c=c: zsb[:, c, :, k3]), [zsbb], by, fbg)
                          pendB.append(item)
                          if len(pendB) > 1:
                              b_conv(pendB.pop(0))
              while pendB:
                  b_conv(pendB.pop(0))
              if has_s or p == NPASS - 1:
                  bt = bank2()
                  pe_transposes([(ps[:, bt * 512 + c * 128: bt * 512 + (c + 1) * 128], zsf[:, c, :], ident[:, :])
                                 for c in range(8)], [zsfb, cst], [bank[bt], bank[bt + 1]])
                  si = stg_next()
                  act(stage[0:SP_ + 2, si, :], ps[0:SP_ + 2, bt * 512:(bt + 2) * 512], AF.Copy,
                      [bank[bt], bank[bt + 1]], [stg[si]])
                  if has_s:
                      out_dma(ncs[l, smp_base:smp_base + SP_, 1, :], stage[0:SP_, si, :], [stg[si]])
                  if p == NPASS - 1:
                      out_dma(ncp[l, :, :], stage[SP_:SP_ + 2, si, :], [stg[si]])

              _ck(7)
              merge_branch(l, 1, HB, w_pb, False, MG)

              if kstep + 1 < NPASS * DEPTH:
                  p2_, l2_ = divmod(kstep + 1, DEPTH)
                  build_layer_consts(l2_, p2_ == 0)

              _ck(8)
              dB2 = Sched.retag(B2all)
              HC = {(c, w): Buf(f"hc{c}_{w}", dB2) for c in range(8) for w in range(3)}
              B2all = list(HC.values())
              slots = [load_dunit(w_in[l, :, C_XC + h * 512: C_XC + (h + 1) * 512]) for h in range(2)]
              swp = load_unit(w_pool[l].rearrange("g k d -> (g k) d"))
              spp, ppv = None, None
              if has_s:
                  spp = state["slot"] % NSLOT
                  state["slot"] += 1
                  ppv = ring[:, spp, :, :].rearrange("p k n -> p (k n)").rearrange("p (h d) -> p h d", h=2)
                  for h in range(2):
                      dma_pool(ppv[0:120, h, :],
                               spool[l, h * 8:(h + 1) * 8, :, :].rearrange("b r d -> (b r) d"), writes=[slotb[spp]])
              _ck(8.1)
              stC = {}

              def c_proj(i, slots=slots, l=l, p=p, smp_base=smp_base):
                  c0, n = tile_cols(i)
                  b = xstat_project(i, slots, [Tb[i]], lambda kc, c0=c0: T[:, kc, c0:c0 + 128])
                  if i == NT128 - 1:
                      cur_ap, cur_b = xccar[:, l, :], xccb[l]
                  else:
                      xi = state["xc"] % 3
                      state["xc"] += 1
                      cur_ap, cur_b = xctm[:, xi, :], xcb[xi]
                  if i == 0:
                      prev = (xccar[:, l, :], xccb[l]) if p > 0 else None
                  else:
                      prev = stC[i - 1][0:2]
                  stC[i] = (cur_ap, cur_b, prev)
                  act(cur_ap, ps[:, b * 512:(b + 2) * 512], AF.Copy, [bank[b], bank[b + 1]], [cur_b])
                  if (i == NT128 - 1 and p == NPASS - 1) or i == 8:
                      si = stg_next()
                      act(stage[:, si, :], ps[:, b * 512:(b + 2) * 512], AF.Copy, [bank[b], bank[b + 1]], [stg[si]])
                      if i == 8:
                          out_dma(nps[l, smp_base:smp_base + SP_, 14, :], stage[0:SP_, si, :], [stg[si]])
                      else:
                          out_dma(npp[l, :, :], stage[113:128, si, :], [stg[si]])

              def c_pool(i, p=p, spp=spp, ppv=ppv):
                  c0, n = tile_cols(i)
                  cur_ap, cur_b, prev = stC[i]
                  bd = bank2()
                  mms = []
                  rd = [cur_b, cst2]
                  for cc in range(8):
                      g = cc // 2
                      o = ps[:, bd * 512 + cc * 128: bd * 512 + cc * 128 + n]
                      if i < NT128:
                          first_tile = (i == 0 and p == 0)
                          kind = 1 if first_tile else 0
                          mms.append((o, cur_ap[:, cc * 128:(cc + 1) * 128], bands[:, g * 3 + kind, :],
                                      True, first_tile))
                          if not first_tile:
                              mms.append((o, prev[0][:, cc * 128:(cc + 1) * 128], bands[:, g * 3 + 2, :],
                                          False, True))
                      else:
                          mms.append((o, ppv[:, 0, cc * 128:(cc + 1) * 128], selb[:, 0, g, :], True, False))
                          mms.append((o, ppv[:, 1, cc * 128:(cc + 1) * 128], selb[:, 1, g, :], False, False))
                          mms.append((o, cur_ap[:, cc * 128:(cc + 1) * 128], i8wb[:, g, :], False, True))
                  if i < NT128 and not (i == 0 and p == 0):
                      rd.append(prev[1])
                  if i == 8:
                      rd.append(slotb[spp])
                  pe_group(mms, rd, [bank[bd], bank[bd + 1]])
                  di = state["dt"] % 2
                  state["dt"] += 1
                  stC[i] = stC[i] + (di,)
                  act(Dt[:, di, :, 0:n], ps[:, bd * 512:(bd + 2) * 512].rearrange("p (c t) -> p c t", c=8)[:, :, 0:n],
                      AF.Copy, [bank[bd], bank[bd + 1]], [Dtb[di]])

              def c_lin(i, l=l, swp=swp):
                  c0, n = tile_cols(i)
                  di = stC[i][3]
                  bh = bank2()
                  mms = []
                  for dj in range(8):
                      g = dj // 2
                      o = ps[:, bh * 512 + dj * 128: bh * 512 + dj * 128 + n]
                      for kk in range(2):
                          mms.append((o, ring[:, swp, g * 2 + kk, (dj % 2) * 128:(dj % 2 + 1) * 128],
                                      Dt[:, di, g * 2 + kk, 0:n], kk == 0, kk == 1))
                  pe_group(mms, [Dtb[di], slotb[swp]], [bank[bh], bank[bh + 1]])
                  w = i // 4 if i < NT128 else 2
                  for dj in range(8):
                      act(hv[:, dj, c0:c0 + n], ps[:, bh * 512 + dj * 128: bh * 512 + dj * 128 + n], AF.Identity,
                          [bank[bh], bank[bh + 1], cst], [HC[(dj, w)]], bias=0.0, scale=pv(l, PV_PS, dj))

              run_pipeline(NTI, [c_proj, c_pool, c_lin], [0, 1, 2])

              _ck(9)
              merge_branch(l, 2, HC, w_pc, False, MG)

              _ck(10)
              mg = B1[:, 0:8 * NTOK].rearrange("p (c t) -> p c t", c=8)
              slots = [load_dunit(w_o[l, :, h * 512:(h + 1) * 512]) for h in range(2)]
              def pre_wo(i, acc, accb, slots=slots):
                  c0, n = tile_cols(i)
                  w = i // 4 if i < NT128 else 2
                  b = xstat_project(i, slots, [MG[(j, w)] for j in range(8)], lambda kc, c0=c0: mg[:, kc, c0:c0 + 128])
                  dve_stt(R[:, i, :], R[:, i, :], ALPHA, ps[:, b * 512:(b + 2) * 512],
                          ALU.mult, ALU.add, [Rb[i], bank[b], bank[b + 1]], [Rb[i], accb], accum_out=acc)
              ln_pipeline(pre_wo, 0, transpose_R_to_T)

              _ck(11)
              av = B1[:, 0:8 * NTOK].rearrange("p (c t) -> p c t", c=8)
              for gq in range(4):
                  dF = Sched.retag(B1all)
                  AG = {(j, w): Buf(f"ag{gq}_{j}_{w}", dF) for j in range(8) for w in range(3)}
                  B1all = list(AG.values())
                  for jp in range(4):
                      s1 = load_unit(w_f1[l, :, gq * 1024 + jp * 256: gq * 1024 + (jp + 1) * 256])
                      for jj in range(2):
                          j = jp * 2 + jj
                          for w, (t0, n) in enumerate(cur["WT"]):
                              bb = bank1()
                              pe_group([(pcols(bb, 0, n), ring[:, s1, kc, jj * 128:(jj + 1) * 128], T[:, kc, t0:t0 + n],
                                         kc == 0, kc == 7) for kc in range(8)],
                                       [slotb[s1]] + [Tb[i] for i in wt_tiles(w)], [bank[bb]])
                              f = ft_next()
                              act(ftmp[:, f, 0:n], pcols(bb, 0, n), AF.Relu, [bank[bb]], [ftb[f]])
                              dve_tt(av[:, j, t0:t0 + n], ftmp[:, f, 0:n], pcols(bb, 0, n), ALU.mult,
                                     [ftb[f], bank[bb]], [AG[(j, w)]])
                  slots = [load_dunit(w_f2[l, gq * 1024:(gq + 1) * 1024, h * 512:(h + 1) * 512]) for h in range(2)]
                  def pre_ff2(i, acc=None, accb=None, slots=slots, AG=AG, gq=gq):
                      c0, n = tile_cols(i)
                      w = i // 4 if i < NT128 else 2
                      b = xstat_project(i, slots, [AG[(j, w)] for j in range(8)], lambda kc, c0=c0: av[:, kc, c0:c0 + 128])
                      if gq == 0:
                          dve_stt(R[:, i, :], R[:, i, :], ALPHA, ps[:, b * 512:(b + 2) * 512],
                                  ALU.mult, ALU.add, [Rb[i], bank[b], bank[b + 1]], [Rb[i]])
                      elif acc is None:
                          dve_tt(R[:, i, :], R[:, i, :], ps[:, b * 512:(b + 2) * 512], ALU.add,
                                 [Rb[i], bank[b], bank[b + 1]], [Rb[i]])
                      else:
                          dve_stt(R[:, i, :], ps[:, b * 512:(b + 2) * 512], 1.0, R[:, i, :], ALU.mult, ALU.add,
                                  [Rb[i], bank[b], bank[b + 1]], [Rb[i], accb], accum_out=acc)

                  def post_ln2(i, l=l, tok_base=tok_base, smp_base=smp_base, p=p):
                      if l < DEPTH - 1:
                          transpose_R_to_T(i)
                      elif i < NT128:
                          out_dma(y_p[tok_base + i * 128: tok_base + (i + 1) * 128, :], R[:, i, :], [Rb[i]])
                          if p + 1 < NPASS:
                              nb_ = tok_base + PT
                              dma_sp(R[:, i, :], x_p[nb_ + i * 128: nb_ + (i + 1) * 128, :], writes=[Rb[i]])
                      else:
                          out_dma(y_s[smp_base:smp_base + SP_, :], R[0:SP_, 8, :], [Rb[8]])

                  if gq < 3:
                      for i in range(NTI):
                          pre_ff2(i)
                  else:
                      ln_pipeline(pre_ff2, 2, post_ln2)
    except _Stop:
        pass

    sems = {}
    for e in Sched.COMPUTE:
        sems[e] = es.enter_context(nc.semaphore(f"prog_{e}"))
    for q in ("sp", "pool"):
        for i in range(Sched.NDMASEM):
            sems[("dma", q, i)] = es.enter_context(nc.semaphore(f"dma_{q}_{i}"))
    final_waits = {}
    for k, n in S.dma_cnt.items():
        if k[1] == "sp":
            final_waits[k] = 16 * n
    block = es.enter_context(nc.Block())
    S.emit_all(nc, block, sems, final_waits)
    es.close()
    return nc


_CACHE = {}


def _get_nc():
    if "nc" not in _CACHE:
        _CACHE["nc"] = build_nc()
    return _CACHE["nc"]


def kernel(x_prompt, x_sample, state_conv, state_pool, w_in, lnv_g, lnv_b, w_spatial, b_spatial,
           w_proj_a, conv_w, conv_b, w_proj_b, w_pool, pool_scale, w_proj_c, w_o,
           ln1_g, ln1_b, w_ff1, w_ff2, ln2_g, ln2_b):
    f = lambda a: np.ascontiguousarray(np.asarray(a, dtype=np.float32))
    x_prompt, x_sample, state_conv, state_pool = f(x_prompt), f(x_sample), f(state_conv), f(state_pool)
    vecs = [f(lnv_g), f(lnv_b), f(conv_w)[:, 0], f(conv_w)[:, 1], f(conv_w)[:, 2], f(conv_b), f(pool_scale)]
    pvec = np.stack([np.stack([v[l].reshape(8, 128).T for v in vecs], axis=1) for l in range(DEPTH)], axis=1)
    pvec = np.ascontiguousarray(pvec.reshape(128, DEPTH * NPV * 8))
    lnbc = np.stack([np.stack([np.broadcast_to(v[l][None, :], (128, D)) for v in
                               (f(ln1_g), f(ln1_b), f(ln2_g), f(ln2_b))]) for l in range(DEPTH)])
    lnbc = np.ascontiguousarray(lnbc)
    wsT = np.ascontiguousarray(f(w_spatial).transpose(0, 3, 1, 2).reshape(DEPTH, 128, 512))
    bsbc = np.ascontiguousarray(np.broadcast_to(f(b_spatial).reshape(DEPTH, 1, 512), (DEPTH, 128, 512)))
    ws00 = np.ascontiguousarray(np.broadcast_to(f(w_spatial)[:, None, :, 0, 0], (DEPTH, 128, 4)))
    bs0 = np.ascontiguousarray(np.broadcast_to(f(b_spatial)[:, None, :, 0], (DEPTH, 128, 4)))
    ident, mask, bands, sel, i8w = _host_consts()
    shared = {
        "w_in": f(w_in), "w_proj_a": f(w_proj_a), "w_proj_b": f(w_proj_b), "w_proj_c": f(w_proj_c),
        "w_o": f(w_o), "w_ff1": f(w_ff1), "w_ff2": f(w_ff2), "w_pool": f(w_pool),
        "pvec": pvec, "lnbc": lnbc, "wsT": wsT, "bsbc": bsbc, "ws00": ws00, "bs0": bs0,
        "c_ident": ident, "c_mask": mask, "c_bands": np.ascontiguousarray(bands.reshape(128, 12 * 128)),
        "c_sel": np.ascontiguousarray(sel.reshape(128, 2 * 4 * SP_)),
        "c_i8w": np.ascontiguousarray(i8w.reshape(128, 4 * SP_)),
    }
    in_maps = []
    for c in range(NCORE):
        m = dict(shared)
        m["x_p"] = np.ascontiguousarray(x_prompt[c])
        m["x_s"] = np.ascontiguousarray(x_sample[c * NSAMP:(c + 1) * NSAMP, 0, :])
        m["sconv"] = np.ascontiguousarray(state_conv[:, c * NSAMP:(c + 1) * NSAMP])
        m["spool"] = np.ascontiguousarray(state_pool[:, c * NSAMP:(c + 1) * NSAMP])
        in_maps.append(m)
    nc = _get_nc()
    res = run_bass_kernel_spmd(nc, in_maps, core_ids=list(range(NCORE)))
    rs = res.results
    y_prompt = np.stack([rs[c]["y_p"] for c in range(NCORE)]).astype(np.float32)
    y_sample = np.concatenate([rs[c]["y_s"] for c in range(NCORE)])[:, None, :].astype(np.float32)
    new_conv_prompt = np.stack([rs[c]["ncp"] for c in range(NCORE)], axis=1).astype(np.float32)
    new_pool_prompt = np.stack([rs[c]["npp"] for c in range(NCORE)], axis=1).astype(np.float32)
    new_conv_sample = np.concatenate([rs[c]["ncs"] for c in range(NCORE)], axis=1).astype(np.float32)
    new_pool_sample = np.concatenate([rs[c]["nps"] for c in range(NCORE)], axis=1).astype(np.float32)
    chunk_v_sample = np.concatenate([rs[c]["cvs"] for c in range(NCORE)], axis=1)[:, :, None, :].astype(np.float32)
    return (y_prompt, y_sample, new_conv_prompt, new_pool_prompt, new_conv_sample, new_pool_sample,
            chunk_v_sample)
```

```python
import numpy as np
from contextlib import ExitStack
import concourse.bass as bass
import concourse.mybir as mybir
from concourse.bass_utils import run_bass_kernel_spmd

F32 = mybir.dt.float32
BF16 = mybir.dt.bfloat16
AF = mybir.ActivationFunctionType
ALU = mybir.AluOpType

D = 1024
DEPTH = 2
NCORE = 8
SEQ = 2048
NSAMP = 16
NPASS = 2
PT = SEQ // NPASS
SP_ = NSAMP
NTOK = PT + 128
NT128 = PT // 128
WT = [(0, 512), (512, 512), (PT, SP_)]
ALPHA = float((2 * DEPTH) ** 0.25)
LN_EPS = 1e-5
POOL_W = (2, 4, 8, 16)
NSLOT = 10
STOP_AFTER = None


class _Stop(Exception):
    pass


def _ck(k):
    if STOP_AFTER is not None and k and k >= STOP_AFTER:
        raise _Stop()
C_U, C_V, C_BG, C_CG, C_XB, C_XC, C_GATE = 0, 1024, 2048, 3072, 4096, 5120, 6144
PV_LNVG, PV_LNVB, PV_CW0, PV_CW1, PV_CW2, PV_CB, PV_PS = range(7)
NPV = 7


class Buf:
    __slots__ = ("name", "lw", "rd")

    def __init__(self, name, init=None):
        self.name = name
        self.lw = dict(init) if init else {}
        self.rd = {}


def _merge(dst, src):
    for k, v in src.items():
        if dst.get(k, -1) < v:
            dst[k] = v


class Op:
    __slots__ = ("emit", "raw", "other", "tok", "dma")


class Sched:
    COMPUTE = ("pe", "act", "dve", "pool")
    NDMASEM = 10

    def __init__(self):
        self.ops = {e: [] for e in ("pe", "act", "dve", "pool", "sp")}
        self.cnt = {e: 0 for e in self.COMPUTE}
        self.dma_n = {"sp": 0, "pool": 0}
        self.dma_cnt = {}

    def add(self, eng, emit, reads=(), writes=(), dma=False):
        op = Op()
        op.emit = emit
        op.dma = dma
        raw, other = {}, {}
        for b in reads:
            _merge(raw, b.lw)
        for b in writes:
            _merge(other, b.lw)
            _merge(other, b.rd)
        if dma:
            i = self.dma_n[eng] % self.NDMASEM
            self.dma_n[eng] += 1
            key = ("dma", eng, i)
            n = self.dma_cnt.get(key, 0)
            if n > 0:
                _merge(other, {key: 16 * n})
            self.dma_cnt[key] = n + 1
            tok = (key, 16 * (n + 1))
        else:
            self.cnt[eng] += 1
            tok = (eng, self.cnt[eng])
        op.raw, op.other, op.tok = raw, other, tok
        t = {tok[0]: tok[1]}
        for b in writes:
            b.lw = dict(t)
            b.rd = {}
        for b in reads:
            _merge(b.rd, t)
        self.ops[eng].append(op)
        return tok

    @staticmethod
    def retag(bufs):
        d = {}
        for b in bufs:
            _merge(d, b.lw)
            _merge(d, b.rd)
        return d

    def emit_all(self, nc, block, sems, final_waits):
        sched = self

        def run(engname, e):
            known = {}
            for op in sched.ops[engname]:
                deps = {}
                for k, v in op.raw.items():
                    if k == engname and engname == "pe":
                        continue
                    deps[k] = max(deps.get(k, -1), v)
                for k, v in op.other.items():
                    if k == engname:
                        continue
                    deps[k] = max(deps.get(k, -1), v)
                for k, v in deps.items():
                    if known.get(k, -1) >= v:
                        continue
                    e.wait_ge(sems[k], v)
                    known[k] = v
                ins = op.emit(e)
                if op.dma:
                    ins.then_inc(sems[op.tok[0]], 16)
                else:
                    ins.then_inc(sems[op.tok[0]], 1)
            if engname == "sp":
                for k, v in final_waits.items():
                    if known.get(k, -1) < v:
                        e.wait_ge(sems[k], v)

        @block.tensor
        def _(e):
            run("pe", e)

        @block.scalar
        def _(e):
            run("act", e)

        @block.vector
        def _(e):
            run("dve", e)

        @block.gpsimd
        def _(e):
            run("pool", e)

        @block.sync
        def _(e):
            run("sp", e)


def _host_consts():
    ident = np.eye(128, dtype=np.float32)
    s = np.arange(128)[:, None]
    t = np.arange(128)[None, :]
    mask = (s <= t).astype(np.float32)
    bands = np.zeros((128, 12, 128), np.float32)
    for g, w in enumerate(POOL_W):
        main = ((s <= t) & (s > t - w)).astype(np.float32) / w - (s == t).astype(np.float32)
        cnt = np.minimum(t + 1, w).astype(np.float32)
        first = ((s <= t) & (s > t - w)).astype(np.float32) / cnt - (s == t).astype(np.float32)
        halo = ((s - 128) > (t - w)).astype(np.float32) / w
        bands[:, g * 3 + 0, :] = main
        bands[:, g * 3 + 1, :] = first
        bands[:, g * 3 + 2, :] = halo
    sel = np.zeros((128, 2, 4, SP_), np.float32)
    i8w = np.zeros((128, 4, SP_), np.float32)
    for g, w in enumerate(POOL_W):
        for b in range(SP_):
            for r in range(15):
                if r >= 15 - (w - 1):
                    sel[(b % 8) * 15 + r, b // 8, g, b] = 1.0 / w
            i8w[b, g, b] = 1.0 / w - 1.0
    return ident, mask, bands, sel, i8w


def build_nc():
    nc = bass.Bass("TRN2", target_bir_lowering=False)
    es = ExitStack()

    def din(name, shape):
        return nc.dram_tensor(name, list(shape), F32, kind="ExternalInput").ap()

    def dout(name, shape):
        return nc.dram_tensor(name, list(shape), F32, kind="ExternalOutput").ap()

    x_p = din("x_p", [SEQ, D])
    x_s = din("x_s", [NSAMP, D])
    sconv = din("sconv", [DEPTH, NSAMP, 2, D])
    spool = din("spool", [DEPTH, NSAMP, 15, D])
    w_in = din("w_in", [DEPTH, D, 9216])
    w_pa = din("w_proj_a", [DEPTH, D, D])
    w_pb = din("w_proj_b", [DEPTH, D, D])
    w_pc = din("w_proj_c", [DEPTH, D, D])
    w_o = din("w_o", [DEPTH, D, D])
    w_f1 = din("w_ff1", [DEPTH, D, 4 * D])
    w_f2 = din("w_ff2", [DEPTH, 4 * D, D])
    w_pool = din("w_pool", [DEPTH, 4, 256, 256])
    pvec_d = din("pvec", [128, DEPTH * NPV * 8])
    lnbc_d = din("lnbc", [DEPTH, 4, 128, D])
    wsT_d = din("wsT", [DEPTH, 128, 4 * 128])
    bsbc_d = din("bsbc", [DEPTH, 128, 4 * 128])
    ws00_d = din("ws00", [DEPTH, 128, 4])
    bs0_d = din("bs0", [DEPTH, 128, 4])
    c_ident = din("c_ident", [128, 128])
    c_mask = din("c_mask", [128, 128])
    c_bands = din("c_bands", [128, 12 * 128])
    c_sel = din("c_sel", [128, 2 * 4 * SP_])
    c_i8w = din("c_i8w", [128, 4 * SP_])

    y_p = dout("y_p", [SEQ, D])
    y_s = dout("y_s", [NSAMP, D])
    ncp = dout("ncp", [DEPTH, 2, D])
    npp = dout("npp", [DEPTH, 15, D])
    ncs = dout("ncs", [DEPTH, NSAMP, 2, D])
    nps = dout("nps", [DEPTH, NSAMP, 15, D])
    cvs = dout("cvs", [DEPTH, NSAMP, D])

    def sb(name, shape, dt=F32):
        return es.enter_context(nc.sbuf_tensor(name, list(shape), dt))

    R = sb("R", [128, 9, D])
    T = sb("T", [128, 8, NTOK], BF16)
    B1 = sb("B1", [128, 9 * D], BF16)
    B2 = sb("B2", [128, 8 * NTOK], BF16)
    ring = sb("ring", [128, NSLOT, 8, 256], BF16)
    lnbc = sb("lnbc_sb", [128, 4, D])
    Dt = sb("Dt", [128, 2, 8, 128], BF16)
    NFT = 6
    ftmp = sb("ftmp", [128, NFT, 512])
    zbuf = sb("zbuf", [128, 3, 514], BF16)
    zsb = sb("zsb", [128, 8, SP_, 3], BF16)
    xctm = sb("xctm", [128, 3, D], BF16)
    xccar = sb("xccar", [128, DEPTH, D], BF16)
    zcar = sb("zcar", [128, DEPTH, 8, 2], BF16)
    stage = sb("stage", [128, 2, D])
    ident = sb("ident", [128, 128])
    ones = sb("ones", [128, 128])
    maskt = sb("maskt", [128, 128])
    pvec = sb("pvec_sb", [128, DEPTH * NPV * 8])
    bands = sb("bands", [128, 12, 128], BF16)
    selb = sb("selb", [128, 2, 4, SP_], BF16)
    i8wb = sb("i8wb", [128, 4, SP_], BF16)
    WsT = sb("WsT", [128, 4, 128], BF16)
    WsS = sb("WsS", [128, 4, SP_], BF16)
    diag = sb("diag", [128, 8, 3, 128], BF16)
    Bias2 = sb("Bias2", [128, 8, 128])
    Bias2s = sb("Bias2s", [128, 8])
    ws00 = sb("ws00_sb", [128, 4])
    bs0 = sb("bs0_sb", [128, 4])
    zsf = sb("zsf", [128, 8, 128])
    vnT = zsf
    NSM = 8
    stat = sb("stat", [128, NSM, 12])
    mv = sb("mv", [128, NSM, 2])
    sm = sb("sm", [128, NSM, 4])
    epsb = sb("epsb", [128, 1])

    ps = es.enter_context(nc.psum_tensor("ps", [128, 8 * 512], F32))

    S = Sched()

    bank = [Buf(f"bank{i}") for i in range(8)]
    slotb = [Buf(f"slot{i}") for i in range(NSLOT)]
    Rb = [Buf(f"R{i}") for i in range(9)]
    Tb = [Buf(f"T{i}") for i in range(9)]
    ftb = [Buf(f"ft{i}") for i in range(NFT)]
    zbb = [Buf("zb0"), Buf("zb1"), Buf("zb2")]
    zsbb = Buf("zsb")
    xcb = [Buf("xc0"), Buf("xc1"), Buf("xc2")]
    xccb = [Buf("xcc0"), Buf("xcc1")]
    zcb = [Buf("zc0"), Buf("zc1")]
    ppb = Buf("pp")
    stg = [Buf("stg0"), Buf("stg1")]
    Dtb = [Buf("Dt0"), Buf("Dt1")]
    lnb = Buf("lnbc")
    cst = Buf("consts")
    cst2 = Buf("consts_pool")
    cst3 = Buf("consts_dve")
    lcst = Buf("layer_consts")
    smallb = [Buf(f"small{i}") for i in range(8)]
    zsfb = Buf("zsf")
    vnb = zsfb
    state = {"bank1": 0, "bank2": 0, "slot": 0, "ft": 0, "stg": 0, "small": 0, "zb": 0, "xc": 0, "dt": 0}

    def bank1():
        i = state["bank1"] % 8
        state["bank1"] += 1
        return i

    def bank2():
        i = state["bank2"] % 4
        state["bank2"] += 1
        return 2 * i

    def pcols(b, a, n):
        return ps[:, b * 512 + a: b * 512 + a + n]

    def ft_next():
        i = state["ft"] % NFT
        state["ft"] += 1
        return i

    def stg_next():
        i = state["stg"] % 2
        state["stg"] += 1
        return i

    def small_next():
        i = state["small"] % 8
        state["small"] += 1
        return i

    def dma_sp(out, in_, reads=(), writes=()):
        return S.add("sp", lambda e: e.dma_start(out=out, in_=in_), reads, writes, dma=True)

    def dma_pool(out, in_, reads=(), writes=()):
        return S.add("pool", lambda e: e.dma_start(out=out, in_=in_), reads, writes, dma=True)

    def load_unit(src2d):
        s = state["slot"] % NSLOT
        state["slot"] += 1
        dma_pool(ring[:, s, :, :], src2d.rearrange("(kc p) n -> p kc n", p=128), writes=[slotb[s]])
        return s

    def load_dunit(src2d):
        if state["slot"] % 2 == 1:
            state["slot"] += 1
        s = state["slot"] % NSLOT
        state["slot"] += 2
        view = ring[:, s:s + 2, :, :].rearrange("p a k n -> p (a k n)").rearrange("p (k n) -> p k n", k=8)
        dma_pool(view, src2d.rearrange("(kc p) n -> p kc n", p=128), writes=[slotb[s], slotb[s + 1]])
        return (s, view)

    def pe_group(mms, reads, writes):
        def emit(e):
            ins = None
            for (o, l, r, st, sp_) in mms:
                ins = e.matmul(o, l, r, start=st, stop=sp_)
            return ins
        return S.add("pe", emit, reads, writes)

    def pe_transposes(items, reads, writes):
        def emit(e):
            ins = None
            for (o, i, idn) in items:
                ins = e.transpose(o, i, idn)
            return ins
        return S.add("pe", emit, reads, writes)

    def act(out, in_, func, reads, writes, bias=None, scale=None, accum_out=None):
        kw = {}
        if bias is not None:
            kw["bias"] = bias
        if scale is not None:
            kw["scale"] = scale
        if accum_out is not None:
            kw["accum_out"] = accum_out
        return S.add("act", lambda e: e.activation(out=out, in_=in_, func=func, **kw), reads, writes)

    def dve_tt(out, in0, in1, op, reads, writes):
        return S.add("dve", lambda e: e.tensor_tensor(out=out, in0=in0, in1=in1, op=op), reads, writes)

    def dve_ts(out, in0, s1, s2, op0, op1, reads, writes):
        if op1 is None:
            return S.add("dve", lambda e: e.tensor_scalar(out=out, in0=in0, scalar1=s1, scalar2=None, op0=op0),
                         reads, writes)
        return S.add("dve", lambda e: e.tensor_scalar(out=out, in0=in0, scalar1=s1, scalar2=s2, op0=op0, op1=op1),
                     reads, writes)

    def dve_stt(out, in0, scalar, in1, op0, op1, reads, writes, accum_out=None):
        if accum_out is not None:
            return S.add("dve", lambda e: e.scalar_tensor_tensor(out=out, in0=in0, scalar=scalar, in1=in1,
                                                                 op0=op0, op1=op1, accum_out=accum_out),
                         reads, writes)
        return S.add("dve", lambda e: e.scalar_tensor_tensor(out=out, in0=in0, scalar=scalar, in1=in1,
                                                             op0=op0, op1=op1), reads, writes)

    def dve_copy(out, in_, reads, writes):
        return S.add("dve", lambda e: e.tensor_copy(out=out, in_=in_), reads, writes)

    def pv(l, v, c):
        i = (l * NPV + v) * 8 + c
        return pvec[:, i:i + 1]

    dma_sp(ident[:, :], c_ident, writes=[cst])
    dma_sp(maskt[:, :], c_mask, writes=[cst])
    dma_sp(pvec[:, :], pvec_d, writes=[cst])
    def load_pool_consts():
        dma_pool(bands[:, :, :], c_bands.rearrange("p (a b) -> p a b", a=12), writes=[cst2])
        dma_pool(selb[:, :, :, :], c_sel.rearrange("p (h a b) -> p h a b", h=2, a=4), writes=[cst2])
        dma_pool(i8wb[:, :, :], c_i8w.rearrange("p (a b) -> p a b", a=4), writes=[cst2])
    S.add("dve", lambda e: e.memset(ones[:, :], 1.0), (), [cst3])
    S.add("dve", lambda e: e.memset(epsb[:, :], LN_EPS), (), [cst3])
    S.add("dve", lambda e: e.memset(zsf[:, :, :], 0.0), (), [zsfb])

    out_tokens = {}

    def passthrough_copies():
        for l_ in range(DEPTH):
            tk = dma_sp(ncs[l_, :, 0, :], sconv[l_, :, 1, :])
            out_tokens[tk[0]] = max(out_tokens.get(tk[0], 0), tk[1])
            for h in range(2):
                tk = dma_sp(nps[l_, h * 8:(h + 1) * 8, 0:14, :], spool[l_, h * 8:(h + 1) * 8, 1:15, :])
                out_tokens[tk[0]] = max(out_tokens.get(tk[0], 0), tk[1])

    def out_dma(out, in_, reads):
        tk = dma_sp(out, in_, reads=reads)
        out_tokens[tk[0]] = max(out_tokens.get(tk[0], 0), tk[1])

    def tile_rows(i):
        return 128

    def tile_cols(i):
        return (i * 128, 128) if i < NT128 else (PT, SP_)

    def wt_tiles(w):
        return list(range(4 * w, 4 * w + 4)) if w < 2 else [8]

    def transpose_R_to_T(i):
        rows = tile_rows(i)
        c0, n = tile_cols(i)
        n = 128
        b = bank2()
        items = []
        for c in range(8):
            items.append((ps[:, b * 512 + c * 128: b * 512 + c * 128 + rows],
                          R[0:rows, i, c * 128:(c + 1) * 128], ident[0:rows, 0:rows]))
        pe_transposes(items, [Rb[i], cst], [bank[b], bank[b + 1]])
        src = ps[:, b * 512:(b + 2) * 512].rearrange("p (c t) -> p c t", c=8)[:, :, 0:rows]
        act(T[:, :, c0:c0 + n], src, AF.Copy, [bank[b], bank[b + 1]], [Tb[i]])

    def xstat_project(i, slots, dst_reads, lhs_of_kc):
        rows = tile_rows(i)
        b = bank2()
        mms = []
        for h in range(2):
            for kc in range(8):
                mms.append((ps[0:rows, (b + h) * 512:(b + h + 1) * 512],
                            lhs_of_kc(kc), slots[h][1][:, kc, :], kc == 0, kc == 7))
        sl = []
        for (s0, _v) in slots:
            sl += [slotb[s0], slotb[s0 + 1]]
        pe_group(mms, dst_reads + sl, [bank[b], bank[b + 1]])
        return b

    def run_pipeline(n_items, stages, lags):
        for t in range(n_items + max(lags)):
            for f, lag in zip(stages, lags):
                i = t - lag
                if 0 <= i < n_items:
                    f(i)

    def ln_pipeline(pre, gi, post):
        st = {}

        def s_stats(i):
            k, sbuf_ = st[i]
            dve_tt(stat[:, k, 2:3], stat[:, k, 0:1], stat[:, k, 0:1], ALU.mult, [sbuf_], [sbuf_])
            dve_stt(mv[:, k, 1:2], stat[:, k, 1:2], float(D), stat[:, k, 2:3], ALU.mult, ALU.subtract,
                    [sbuf_], [sbuf_])
            act(sm[:, k, 0:1], mv[:, k, 1:2], AF.Sqrt, [sbuf_, cst3], [sbuf_], bias=epsb[:, :],
                scale=1.0 / (float(D) * float(D)))

        def s_rstd(i):
            k, sbuf_ = st[i]
            S.add("dve", lambda e: e.reciprocal(out=sm[:, k, 1:2], in_=sm[:, k, 0:1]), [sbuf_], [sbuf_])
            dve_stt(sm[:, k, 2:3], stat[:, k, 0:1], -1.0 / D, sm[:, k, 1:2], ALU.mult, ALU.mult, [sbuf_], [sbuf_])
            act(R[:, i, :], R[:, i, :], AF.Identity, [Rb[i], sbuf_], [Rb[i]], bias=sm[:, k, 2:3], scale=sm[:, k, 1:2])

        def s_affine(i):
            dve_tt(R[:, i, :], R[:, i, :], lnbc[:, gi, :], ALU.mult, [Rb[i], lnb], [Rb[i]])
            dve_tt(R[:, i, :], R[:, i, :], lnbc[:, gi + 1, :], ALU.add, [Rb[i], lnb], [Rb[i]])

        def s_pre(i):
            k = small_next()
            sbuf_ = smallb[k]
            st[i] = (k, sbuf_)
            pre(i, stat[:, k, 0:1], sbuf_)
            sj = stg_next()
            act(stage[:, sj, :], R[:, i, :], AF.Square, [Rb[i]], [stg[sj], sbuf_], accum_out=stat[:, k, 1:2])

        run_pipeline(cur["NTI"], [s_pre, s_affine, s_stats, s_rstd, post], [0, 2, 0, 1, 3])

    def merge_branch(l, br, hbufs, wproj, first, Mb_new):
        hv = B2[:, :].rearrange("p (c t) -> p c t", c=8)
        mg = B1[:, 0:8 * NTOK].rearrange("p (c t) -> p c t", c=8)
        items = []
        units = {}

        def g_stage(k):
            jp, jj, w, t0, n = items[k]
            if jp not in units:
                units[jp] = (load_unit(wproj[l, :, jp * 256:(jp + 1) * 256]),
                             load_unit(w_in[l, :, C_GATE + br * D + jp * 256: C_GATE + br * D + (jp + 1) * 256]))
            sg = units[jp][1]
            bg_ = bank1()
            pe_group([(pcols(bg_, 0, n), ring[:, sg, kc, jj * 128:(jj + 1) * 128], T[:, kc, t0:t0 + n],
                       kc == 0, kc == 7) for kc in range(8)],
                     [slotb[sg]] + [Tb[i] for i in wt_tiles(w)], [bank[bg_]])
            f = ft_next()
            act(ftmp[:, f, 0:n], pcols(bg_, 0, n), AF.Sigmoid, [bank[bg_]], [ftb[f]])
            items[k] = (jp, jj, w, t0, n, f)

        def p_stage(k):
            jp, jj, w, t0, n, f = items[k]
            j = jp * 2 + jj
            sp_ = units[jp][0]
            bp = bank1()
            pe_group([(pcols(bp, 0, n), ring[:, sp_, kc, jj * 128:(jj + 1) * 128], hv[:, kc, t0:t0 + n],
                       kc == 0, kc == 7) for kc in range(8)],
                     [slotb[sp_]] + [hbufs[(kc, w)] for kc in range(8)], [bank[bp]])
            mb = Mb_new[(j, w)]
            if first:
                dve_tt(mg[:, j, t0:t0 + n], ftmp[:, f, 0:n], pcols(bp, 0, n), ALU.mult,
                       [ftb[f], bank[bp]], [mb])
            else:
                f2 = ft_next()
                dve_tt(ftmp[:, f2, 0:n], ftmp[:, f, 0:n], pcols(bp, 0, n), ALU.mult,
                       [ftb[f], bank[bp]], [ftb[f2]])
                dve_tt(mg[:, j, t0:t0 + n], mg[:, j, t0:t0 + n], ftmp[:, f2, 0:n], ALU.add,
                       [ftb[f2], mb], [mb])

        for jp in range(4):
            for jj in range(2):
                for w, (t0, n) in enumerate(cur["WT"]):
                    items.append((jp, jj, w, t0, n))
        run_pipeline(len(items), [g_stage, p_stage], [0, 2])

    def build_layer_consts(l2, has_s2):
        si = stg_next()
        dma_sp(stage[:, si, 0:512], wsT_d[l2, :, :], writes=[stg[si]])
        si2 = stg_next()
        dma_sp(stage[:, si2, 0:512], bsbc_d[l2, :, :], writes=[stg[si2]])
        dma_sp(ws00[:, :], ws00_d[l2, :, :], writes=[lcst])
        dma_sp(bs0[:, :], bs0_d[l2, :, :], writes=[lcst])
        st3 = stage[:, si, 0:512].rearrange("p (g t) -> p g t", g=4)
        for g in range(4):
            dve_tt(st3[:, g, :], st3[:, g, :], maskt[:, :], ALU.mult, [stg[si], cst], [stg[si]])
        dve_copy(WsT[:, :, :], st3, [stg[si]], [lcst])
        br_ = bank1()
        pe_group([(pcols(br_, 0, 512), ones[:, :], stage[:, si, 0:512], True, True)],
                 [stg[si], cst3], [bank[br_]])
        bs3 = stage[:, si2, 0:512].rearrange("p (g t) -> p g t", g=4)
        for c in range(8):
            g = c // 2
            dve_stt(Bias2[:, c, :], pcols(br_, g * 128, 128), pv(l2, PV_LNVB, c), bs3[:, g, :],
                    ALU.mult, ALU.add, [bank[br_], stg[si2], cst], [lcst])
            dve_stt(Bias2s[:, c:c + 1], ws00[:, g:g + 1], pv(l2, PV_LNVB, c), bs0[:, g:g + 1],
                    ALU.mult, ALU.add, [lcst, cst], [lcst])
            for k3 in range(3):
                dve_ts(diag[:, c, k3, :], ident[:, :], pv(l2, PV_CW0 + k3, c), None, ALU.mult, None,
                       [cst], [lcst])
        for g in range(4):
            dve_ts(WsS[:, g, :], ident[:, 0:SP_], ws00[:, g:g + 1], None, ALU.mult, None,
                   [cst, lcst], [lcst])
        if has_s2:
            si = stg_next()
            dma_sp(stage[0:2 * SP_, si, :],
                   sconv[l2, 0:SP_, :, :].rearrange("b k d -> (b k) d"), writes=[stg[si]])
            bz = bank2()
            pe_transposes([(pcols(bz, c * 128, 128), stage[:, si, c * 128:(c + 1) * 128], ident[:, :])
                           for c in range(8)], [stg[si], cst], [bank[bz], bank[bz + 1]])
            for c in range(8):
                dve_copy(zsb[:, c, :, 0:2], pcols(bz, c * 128, 2 * SP_).rearrange("p (b k) -> p b k", k=2),
                         [bank[bz], bank[bz + 1]], [zsbb])

    cur = {"NTI": 9, "WT": WT}
    B1all = [Buf("B1init")]
    B2all = [Buf("B2init")]

    try:
      _ck(0)
      for p in range(NPASS):
          tok_base = p * PT
          smp_base = 0
          has_s = (p == 0)
          NTI = 9 if has_s else 8
          WTp = WT if has_s else WT[:2]
          cur["NTI"], cur["WT"] = NTI, WTp
          if p == 0:
              for i in range(NT128):
                  dma_sp(R[:, i, :], x_p[tok_base + i * 128: tok_base + (i + 1) * 128, :], writes=[Rb[i]])
          if has_s:
              S.add("dve", lambda e: e.memset(R[:, 8, :], 0.0), (), [Rb[8]])
              dma_sp(R[0:SP_, 8, :], x_s[smp_base:smp_base + SP_, :], writes=[Rb[8]])
          _ck(0.5)
          for i in range(NTI):
              transpose_R_to_T(i)
              _ck(0.6 + 0.01 * i)

          _ck(1)
          for l in range(DEPTH):
              kstep = p * DEPTH + l

              _ck(2)
              dA = Sched.retag(B1all)
              XH = [Buf(f"xh{i}", dA) for i in range(9)]
              B1all = XH
              xh = B1[:, :].rearrange("p (i d) -> p i d", i=9)
              slots = [load_dunit(w_in[l, :, C_V + h * 512: C_V + (h + 1) * 512]) for h in range(2)]
              _ck(2.1)
              stA = {}

              def a1_proj(i, slots=slots):
                  c0, n = tile_cols(i)
                  b = xstat_project(i, slots, [Tb[i]], lambda kc, c0=c0: T[:, kc, c0:c0 + 128])
                  k = small_next()
                  sbuf_ = smallb[k]
                  stA[i] = (b, k, sbuf_)
                  a0, a1 = ps[:, b * 512:(b + 1) * 512], ps[:, (b + 1) * 512:(b + 2) * 512]
                  S.add("dve", lambda e: e.bn_stats(out=stat[:, k, 0:6], in_=a0), [bank[b]], [sbuf_])
                  S.add("dve", lambda e: e.bn_stats(out=stat[:, k, 6:12], in_=a1), [bank[b + 1]], [sbuf_])
                  S.add("dve", lambda e: e.bn_aggr(out=mv[:, k, :], in_=stat[:, k, :]), [sbuf_], [sbuf_])
                  act(sm[:, k, 0:1], mv[:, k, 1:2], AF.Sqrt, [sbuf_, cst3], [sbuf_], bias=epsb[:, :], scale=1.0)

              def a1_norm(i):
                  b, k, sbuf_ = stA[i]
                  S.add("dve", lambda e: e.reciprocal(out=sm[:, k, 1:2], in_=sm[:, k, 0:1]), [sbuf_], [sbuf_])
                  dve_stt(sm[:, k, 2:3], mv[:, k, 0:1], -1.0, sm[:, k, 1:2], ALU.mult, ALU.mult, [sbuf_], [sbuf_])
                  act(xh[:, i, :], ps[:, b * 512:(b + 2) * 512], AF.Identity,
                      [bank[b], bank[b + 1], sbuf_], [XH[i]], bias=sm[:, k, 2:3], scale=sm[:, k, 1:2])
                  if i == 8:
                      sx = stg_next()
                      stA["xhs"] = sx
                      act(stage[:, sx, :], ps[:, b * 512:(b + 2) * 512], AF.Identity,
                          [bank[b], bank[b + 1], sbuf_], [stg[sx]], bias=sm[:, k, 2:3], scale=sm[:, k, 1:2])

              run_pipeline(NTI, [a1_proj, a1_norm], [0, 1])
              _ck(3)
              if has_s:
                  bt = bank2()
                  sx = stA["xhs"]
                  pe_transposes([(pcols(bt, c * 128, 128), stage[:, sx, c * 128:(c + 1) * 128], ident[:, :])
                                 for c in range(8)], [stg[sx], cst], [bank[bt], bank[bt + 1]])
                  for c in range(8):
                      act(vnT[:, c, 0:SP_], pcols(bt, c * 128, SP_), AF.Identity, [bank[bt], bank[bt + 1], cst], [vnb],
                          bias=pv(l, PV_LNVB, c), scale=pv(l, PV_LNVG, c))
                  bt2 = bank2()
                  pe_transposes([(ps[:, bt2 * 512 + c * 128: bt2 * 512 + (c + 1) * 128], vnT[:, c, :], ident[:, :])
                                 for c in range(8)], [vnb, cst], [bank[bt2], bank[bt2 + 1]])
                  si = stg_next()
                  act(stage[0:SP_, si, :], ps[0:SP_, bt2 * 512:(bt2 + 2) * 512], AF.Copy,
                      [bank[bt2], bank[bt2 + 1]], [stg[si]])
                  out_dma(cvs[l, smp_base:smp_base + SP_, :], stage[0:SP_, si, :], [stg[si]])

              if kstep == 0:
                  build_layer_consts(l, has_s)
              for v in range(4):
                  dma_sp(lnbc[:, v, :], lnbc_d[l, v, :, :], writes=[lnb])
              if kstep == 0:
                  passthrough_copies()
                  load_pool_consts()

              _ck(4)
              dB2 = Sched.retag(B2all)
              HA = {(c, w): Buf(f"ha{c}_{w}", dB2) for c in range(8) for w in range(3)}
              B2all = list(HA.values())
              hv = B2[:, :].rearrange("p (c t) -> p c t", c=8)
              for cp in range(4):
                  su = load_unit(w_in[l, :, C_U + cp * 256: C_U + (cp + 1) * 256])
                  for cc in range(2):
                      c = cp * 2 + cc
                      g = c // 2
                      for w, (t0, n) in enumerate(cur["WT"]):
                          bu = bank1()
                          pe_group([(pcols(bu, 0, n), ring[:, su, kc, cc * 128:(cc + 1) * 128], T[:, kc, t0:t0 + n],
                                     kc == 0, kc == 7) for kc in range(8)],
                                   [slotb[su]] + [Tb[i] for i in wt_tiles(w)], [bank[bu]])
                          bs_ = bank1()
                          f = ft_next()
                          if w < 2:
                              pe_group([(pcols(bs_, j * 128, 128), xh[:, 4 * w + j, c * 128:(c + 1) * 128],
                                         WsT[:, g, :], True, True) for j in range(4)],
                                       [XH[4 * w + j] for j in range(4)] + [lcst], [bank[bs_]])
                              dve_stt(ftmp[:, f, :].rearrange("p (j t) -> p j t", j=4),
                                      pcols(bs_, 0, 512).rearrange("p (j t) -> p j t", j=4),
                                      pv(l, PV_LNVG, c),
                                      Bias2[:, c:c + 1, :].to_broadcast([128, 4, 128]),
                                      ALU.mult, ALU.add, [bank[bs_], lcst, cst], [ftb[f]])
                          else:
                              pe_group([(pcols(bs_, 0, SP_), xh[:, 8, c * 128:(c + 1) * 128],
                                         WsS[:, g, :], True, True)], [XH[8], lcst], [bank[bs_]])
                              dve_stt(ftmp[:, f, 0:SP_], pcols(bs_, 0, SP_), pv(l, PV_LNVG, c),
                                      Bias2s[:, c:c + 1].to_broadcast([128, SP_]),
                                      ALU.mult, ALU.add, [bank[bs_], lcst, cst], [ftb[f]])
                          dve_tt(hv[:, c, t0:t0 + n], ftmp[:, f, 0:n], pcols(bu, 0, n), ALU.mult,
                                 [ftb[f], bank[bu]], [HA[(c, w)]])

              _ck(5)
              dM = Sched.retag(B1all)
              MG = {(j, w): Buf(f"mg{j}_{w}", dM) for j in range(8) for w in range(3)}
              B1all = list(MG.values())
              merge_branch(l, 0, HA, w_pa, True, MG)

              _ck(6)
              dB2 = Sched.retag(B2all)
              HB = {(c, w): Buf(f"hb{c}_{w}", dB2) for c in range(8) for w in range(3)}
              B2all = list(HB.values())
              pendB = []

              def b_conv(item):
                  (c, w, t0, n, zsrc, zbufs, by, fbg) = item
                  pe_group([(pcols(by, 0, n), diag[:, c, k3, :], zsrc(k3), k3 == 0, k3 == 2)
                            for k3 in range(3)], zbufs + [lcst], [bank[by]])
                  dve_stt(hv[:, c, t0:t0 + n], pcols(by, 0, n), pv(l, PV_CB, c), ftmp[:, fbg, 0:n],
                          ALU.add, ALU.mult, [bank[by], ftb[fbg], cst], [HB[(c, w)]])

              for cp in range(4):
                  sbg = load_unit(w_in[l, :, C_BG + cp * 256: C_BG + (cp + 1) * 256])
                  scg = load_unit(w_in[l, :, C_CG + cp * 256: C_CG + (cp + 1) * 256])
                  sxb = load_unit(w_in[l, :, C_XB + cp * 256: C_XB + (cp + 1) * 256])
                  for cc in range(2):
                      c = cp * 2 + cc
                      prev_z = None
                      for w, (t0, n) in enumerate(cur["WT"]):
                          trd = [Tb[i] for i in wt_tiles(w)]
                          bks = []
                          for su_ in (sbg, scg, sxb):
                              bb = bank1()
                              pe_group([(pcols(bb, 0, n), ring[:, su_, kc, cc * 128:(cc + 1) * 128],
                                         T[:, kc, t0:t0 + n], kc == 0, kc == 7) for kc in range(8)],
                                       [slotb[su_]] + trd, [bank[bb]])
                              bks.append(bb)
                          bbg, bcg, bxb = bks
                          fcg = ft_next()
                          act(ftmp[:, fcg, 0:n], pcols(bcg, 0, n), AF.Copy, [bank[bcg]], [ftb[fcg]])
                          fbg = ft_next()
                          act(ftmp[:, fbg, 0:n], pcols(bbg, 0, n), AF.Copy, [bank[bbg]], [ftb[fbg]])
                          by = bank1()
                          if w < 2:
                              zi = state["zb"] % 3
                              state["zb"] += 1
                              dve_tt(zbuf[:, zi, 2:2 + n], ftmp[:, fcg, 0:n], pcols(bxb, 0, n), ALU.mult,
                                     [ftb[fcg], bank[bxb]], [zbb[zi]])
                              if w == 0:
                                  if p == 0:
                                      S.add("dve", lambda e, zi=zi: e.memset(zbuf[:, zi, 0:2], 0.0), (), [zbb[zi]])
                                  else:
                                      dve_copy(zbuf[:, zi, 0:2], zcar[:, l, c, :], [zcb[l]], [zbb[zi]])
                              else:
                                  dve_copy(zbuf[:, zi, 0:2], zbuf[:, prev_z, 512:514], [zbb[prev_z]], [zbb[zi]])
                                  if p == 0:
                                      dve_copy(zcar[:, l, c, :], zbuf[:, zi, 512:514], [zbb[zi]], [zcb[l]])
                                  else:
                                      dve_tt(zsf[:, c, SP_:SP_ + 2], ftmp[:, fcg, n - 2:n], pcols(bxb, n - 2, 2),
                                             ALU.mult, [ftb[fcg], bank[bxb]], [zsfb])
                              prev_z = zi
                              item = (c, w, t0, n, (lambda k3, zi=zi, n=n: zbuf[:, zi, k3:k3 + n]), [zbb[zi]], by, fbg)
                          else:
                              dve_tt(zsb[:, c, :, 2], ftmp[:, fcg, 0:n], pcols(bxb, 0, n), ALU.mult,
                                     [ftb[fcg], bank[bxb]], [zsbb])
                              dve_tt(zsf[:, c, 0:SP_], ftmp[:, fcg, 0:n], pcols(bxb, 0, n), ALU.mult,
                                     [ftb[fcg], bank[bxb]], [zsfb])
                              item = (c, w, t0, n, (lambda k3, c=c: zsb[:, c, :, k3]), [zsbb], by, fbg)
                          pendB.append(item)
                          if len(pendB) > 1:
                              b_conv(pendB.pop(0))
              while pendB:
                  b_conv(pendB.pop(0))
              if has_s or p == NPASS - 1:
                  bt = bank2()
                  pe_transposes([(ps[:, bt * 512 + c * 128: bt * 512 + (c + 1) * 128], zsf[:, c, :], ident[:, :])
                                 for c in range(8)], [zsfb, cst], [bank[bt], bank[bt + 1]])
                  si = stg_next()
                  act(stage[0:SP_ + 2, si, :], ps[0:SP_ + 2, bt * 512:(bt + 2) * 512], AF.Copy,
                      [bank[bt], bank[bt + 1]], [stg[si]])
                  if has_s:
                      out_dma(ncs[l, smp_base:smp_base + SP_, 1, :], stage[0:SP_, si, :], [stg[si]])
                  if p == NPASS - 1:
                      out_dma(ncp[l, :, :], stage[SP_:SP_ + 2, si, :], [stg[si]])

              _ck(7)
              merge_branch(l, 1, HB, w_pb, False, MG)

              if kstep + 1 < NPASS * DEPTH:
                  p2_, l2_ = divmod(kstep + 1, DEPTH)
                  build_layer_consts(l2_, p2_ == 0)

              _ck(8)
              dB2 = Sched.retag(B2all)
              HC = {(c, w): Buf(f"hc{c}_{w}", dB2) for c in range(8) for w in range(3)}
              B2all = list(HC.values())
              slots = [load_dunit(w_in[l, :, C_XC + h * 512: C_XC + (h + 1) * 512]) for h in range(2)]
              swp = load_unit(w_pool[l].rearrange("g k d -> (g k) d"))
              spp, ppv = None, None
              if has_s:
                  spp = state["slot"] % NSLOT
                  state["slot"] += 1
                  ppv = ring[:, spp, :, :].rearrange("p k n -> p (k n)").rearrange("p (h d) -> p h d", h=2)
                  for h in range(2):
                      dma_pool(ppv[0:120, h, :],
                               spool[l, h * 8:(h + 1) * 8, :, :].rearrange("b r d -> (b r) d"), writes=[slotb[spp]])
              _ck(8.1)
              stC = {}

              def c_proj(i, slots=slots, l=l, p=p, smp_base=smp_base):
                  c0, n = tile_cols(i)
                  b = xstat_project(i, slots, [Tb[i]], lambda kc, c0=c0: T[:, kc, c0:c0 + 128])
                  if i == NT128 - 1:
                      cur_ap, cur_b = xccar[:, l, :], xccb[l]
                  else:
                      xi = state["xc"] % 3
                      state["xc"] += 1
                      cur_ap, cur_b = xctm[:, xi, :], xcb[xi]
                  if i == 0:
                      prev = (xccar[:, l, :], xccb[l]) if p > 0 else None
                  else:
                      prev = stC[i - 1][0:2]
                  stC[i] = (cur_ap, cur_b, prev)
                  act(cur_ap, ps[:, b * 512:(b + 2) * 512], AF.Copy, [bank[b], bank[b + 1]], [cur_b])
                  if (i == NT128 - 1 and p == NPASS - 1) or i == 8:
                      si = stg_next()
                      act(stage[:, si, :], ps[:, b * 512:(b + 2) * 512], AF.Copy, [bank[b], bank[b + 1]], [stg[si]])
                      if i == 8:
                          out_dma(nps[l, smp_base:smp_base + SP_, 14, :], stage[0:SP_, si, :], [stg[si]])
                      else:
                          out_dma(npp[l, :, :], stage[113:128, si, :], [stg[si]])

              def c_pool(i, p=p, spp=spp, ppv=ppv):
                  c0, n = tile_cols(i)
                  cur_ap, cur_b, prev = stC[i]
                  bd = bank2()
                  mms = []
                  rd = [cur_b, cst2]
                  for cc in range(8):
                      g = cc // 2
                      o = ps[:, bd * 512 + cc * 128: bd * 512 + cc * 128 + n]
                      if i < NT128:
                          first_tile = (i == 0 and p == 0)
                          kind = 1 if first_tile else 0
                          mms.append((o, cur_ap[:, cc * 128:(cc + 1) * 128], bands[:, g * 3 + kind, :],
                                      True, first_tile))
                          if not first_tile:
                              mms.append((o, prev[0][:, cc * 128:(cc + 1) * 128], bands[:, g * 3 + 2, :],
                                          False, True))
                      else:
                          mms.append((o, ppv[:, 0, cc * 128:(cc + 1) * 128], selb[:, 0, g, :], True, False))
                          mms.append((o, ppv[:, 1, cc * 128:(cc + 1) * 128], selb[:, 1, g, :], False, False))
                          mms.append((o, cur_ap[:, cc * 128:(cc + 1) * 128], i8wb[:, g, :], False, True))
                  if i < NT128 and not (i == 0 and p == 0):
                      rd.append(prev[1])
                  if i == 8:
                      rd.append(slotb[spp])
                  pe_group(mms, rd, [bank[bd], bank[bd + 1]])
                  di = state["dt"] % 2
                  state["dt"] += 1
                  stC[i] = stC[i] + (di,)
                  act(Dt[:, di, :, 0:n], ps[:, bd * 512:(bd + 2) * 512].rearrange("p (c t) -> p c t", c=8)[:, :, 0:n],
                      AF.Copy, [bank[bd], bank[bd + 1]], [Dtb[di]])

              def c_lin(i, l=l, swp=swp):
                  c0, n = tile_cols(i)
                  di = stC[i][3]
                  bh = bank2()
                  mms = []
                  for dj in range(8):
                      g = dj // 2
                      o = ps[:, bh * 512 + dj * 128: bh * 512 + dj * 128 + n]
                      for kk in range(2):
                          mms.append((o, ring[:, swp, g * 2 + kk, (dj % 2) * 128:(dj % 2 + 1) * 128],
                                      Dt[:, di, g * 2 + kk, 0:n], kk == 0, kk == 1))
                  pe_group(mms, [Dtb[di], slotb[swp]], [bank[bh], bank[bh + 1]])
                  w = i // 4 if i < NT128 else 2
                  for dj in range(8):
                      act(hv[:, dj, c0:c0 + n], ps[:, bh * 512 + dj * 128: bh * 512 + dj * 128 + n], AF.Identity,
                          [bank[bh], bank[bh + 1], cst], [HC[(dj, w)]], bias=0.0, scale=pv(l, PV_PS, dj))

              run_pipeline(NTI, [c_proj, c_pool, c_lin], [0, 1, 2])

              _ck(9)
              merge_branch(l, 2, HC, w_pc, False, MG)

              _ck(10)
              mg = B1[:, 0:8 * NTOK].rearrange("p (c t) -> p c t", c=8)
              slots = [load_dunit(w_o[l, :, h * 512:(h + 1) * 512]) for h in range(2)]
              def pre_wo(i, acc, accb, slots=slots):
                  c0, n = tile_cols(i)
                  w = i // 4 if i < NT128 else 2
                  b = xstat_project(i, slots, [MG[(j, w)] for j in range(8)], lambda kc, c0=c0: mg[:, kc, c0:c0 + 128])
                  dve_stt(R[:, i, :], R[:, i, :], ALPHA, ps[:, b * 512:(b + 2) * 512],
                          ALU.mult, ALU.add, [Rb[i], bank[b], bank[b + 1]], [Rb[i], accb], accum_out=acc)
              ln_pipeline(pre_wo, 0, transpose_R_to_T)

              _ck(11)
              av = B1[:, 0:8 * NTOK].rearrange("p (c t) -> p c t", c=8)
              for gq in range(4):
                  dF = Sched.retag(B1all)
                  AG = {(j, w): Buf(f"ag{gq}_{j}_{w}", dF) for j in range(8) for w in range(3)}
                  B1all = list(AG.values())
                  for jp in range(4):
                      s1 = load_unit(w_f1[l, :, gq * 1024 + jp * 256: gq * 1024 + (jp + 1) * 256])
                      for jj in range(2):
                          j = jp * 2 + jj
                          for w, (t0, n) in enumerate(cur["WT"]):
                              bb = bank1()
                              pe_group([(pcols(bb, 0, n), ring[:, s1, kc, jj * 128:(jj + 1) * 128], T[:, kc, t0:t0 + n],
                                         kc == 0, kc == 7) for kc in range(8)],
                                       [slotb[s1]] + [Tb[i] for i in wt_tiles(w)], [bank[bb]])
                              f = ft_next()
                              act(ftmp[:, f, 0:n], pcols(bb, 0, n), AF.Relu, [bank[bb]], [ftb[f]])
                              dve_tt(av[:, j, t0:t0 + n], ftmp[:, f, 0:n], pcols(bb, 0, n), ALU.mult,
                                     [ftb[f], bank[bb]], [AG[(j, w)]])
                  slots = [load_dunit(w_f2[l, gq * 1024:(gq + 1) * 1024, h * 512:(h + 1) * 512]) for h in range(2)]
                  def pre_ff2(i, acc=None, accb=None, slots=slots, AG=AG, gq=gq):
                      c0, n = tile_cols(i)
                      w = i // 4 if i < NT128 else 2
                      b = xstat_project(i, slots, [AG[(j, w)] for j in range(8)], lambda kc, c0=c0: av[:, kc, c0:c0 + 128])
                      if gq == 0:
                          dve_stt(R[:, i, :], R[:, i, :], ALPHA, ps[:, b * 512:(b + 2) * 512],
                                  ALU.mult, ALU.add, [Rb[i], bank[b], bank[b + 1]], [Rb[i]])
                      elif acc is None:
                          dve_tt(R[:, i, :], R[:, i, :], ps[:, b * 512:(b + 2) * 512], ALU.add,
                                 [Rb[i], bank[b], bank[b + 1]], [Rb[i]])
                      else:
                          dve_stt(R[:, i, :], ps[:, b * 512:(b + 2) * 512], 1.0, R[:, i, :], ALU.mult, ALU.add,
                                  [Rb[i], bank[b], bank[b + 1]], [Rb[i], accb], accum_out=acc)

                  def post_ln2(i, l=l, tok_base=tok_base, smp_base=smp_base, p=p):
                      if l < DEPTH - 1:
                          transpose_R_to_T(i)
                      elif i < NT128:
                          out_dma(y_p[tok_base + i * 128: tok_base + (i + 1) * 128, :], R[:, i, :], [Rb[i]])
                          if p + 1 < NPASS:
                              nb_ = tok_base + PT
                              dma_sp(R[:, i, :], x_p[nb_ + i * 128: nb_ + (i + 1) * 128, :], writes=[Rb[i]])
                      else:
                          out_dma(y_s[smp_base:smp_base + SP_, :], R[0:SP_, 8, :], [Rb[8]])

                  if gq < 3:
                      for i in range(NTI):
                          pre_ff2(i)
                  else:
                      ln_pipeline(pre_ff2, 2, post_ln2)
    except _Stop:
        pass

    sems = {}
    for e in Sched.COMPUTE:
        sems[e] = es.enter_context(nc.semaphore(f"prog_{e}"))
    for q in ("sp", "pool"):
        for i in range(Sched.NDMASEM):
            sems[("dma", q, i)] = es.enter_context(nc.semaphore(f"dma_{q}_{i}"))
    final_waits = {}
    for k, n in S.dma_cnt.items():
        if k[1] == "sp":
            final_waits[k] = 16 * n
    block = es.enter_context(nc.Block())
    S.emit_all(nc, block, sems, final_waits)
    es.close()
    return nc


_CACHE = {}


def _get_nc():
    if "nc" not in _CACHE:
        _CACHE["nc"] = build_nc()
    return _CACHE["nc"]


def kernel(x_prompt, x_sample, state_conv, state_pool, w_in, lnv_g, lnv_b, w_spatial, b_spatial,
           w_proj_a, conv_w, conv_b, w_proj_b, w_pool, pool_scale, w_proj_c, w_o,
           ln1_g, ln1_b, w_ff1, w_ff2, ln2_g, ln2_b):
    f = lambda a: np.ascontiguousarray(np.asarray(a, dtype=np.float32))
    x_prompt, x_sample, state_conv, state_pool = f(x_prompt), f(x_sample), f(state_conv), f(state_pool)
    vecs = [f(lnv_g), f(lnv_b), f(conv_w)[:, 0], f(conv_w)[:, 1], f(conv_w)[:, 2], f(conv_b), f(pool_scale)]
    pvec = np.stack([np.stack([v[l].reshape(8, 128).T for v in vecs], axis=1) for l in range(DEPTH)], axis=1)
    pvec = np.ascontiguousarray(pvec.reshape(128, DEPTH * NPV * 8))
    lnbc = np.stack([np.stack([np.broadcast_to(v[l][None, :], (128, D)) for v in
                               (f(ln1_g), f(ln1_b), f(ln2_g), f(ln2_b))]) for l in range(DEPTH)])
    lnbc = np.ascontiguousarray(lnbc)
    wsT = np.ascontiguousarray(f(w_spatial).transpose(0, 3, 1, 2).reshape(DEPTH, 128, 512))
    bsbc = np.ascontiguousarray(np.broadcast_to(f(b_spatial).reshape(DEPTH, 1, 512), (DEPTH, 128, 512)))
    ws00 = np.ascontiguousarray(np.broadcast_to(f(w_spatial)[:, None, :, 0, 0], (DEPTH, 128, 4)))
    bs0 = np.ascontiguousarray(np.broadcast_to(f(b_spatial)[:, None, :, 0], (DEPTH, 128, 4)))
    ident, mask, bands, sel, i8w = _host_consts()
    shared = {
        "w_in": f(w_in), "w_proj_a": f(w_proj_a), "w_proj_b": f(w_proj_b), "w_proj_c": f(w_proj_c),
        "w_o": f(w_o), "w_ff1": f(w_ff1), "w_ff2": f(w_ff2), "w_pool": f(w_pool),
        "pvec": pvec, "lnbc": lnbc, "wsT": wsT, "bsbc": bsbc, "ws00": ws00, "bs0": bs0,
        "c_ident": ident, "c_mask": mask, "c_bands": np.ascontiguousarray(bands.reshape(128, 12 * 128)),
        "c_sel": np.ascontiguousarray(sel.reshape(128, 2 * 4 * SP_)),
        "c_i8w": np.ascontiguousarray(i8w.reshape(128, 4 * SP_)),
    }
    in_maps = []
    for c in range(NCORE):
        m = dict(shared)
        m["x_p"] = np.ascontiguousarray(x_prompt[c])
        m["x_s"] = np.ascontiguousarray(x_sample[c * NSAMP:(c + 1) * NSAMP, 0, :])
        m["sconv"] = np.ascontiguousarray(state_conv[:, c * NSAMP:(c + 1) * NSAMP])
        m["spool"] = np.ascontiguousarray(state_pool[:, c * NSAMP:(c + 1) * NSAMP])
        in_maps.append(m)
    nc = _get_nc()
    res = run_bass_kernel_spmd(nc, in_maps, core_ids=list(range(NCORE)))
    rs = res.results
    y_prompt = np.stack([rs[c]["y_p"] for c in range(NCORE)]).astype(np.float32)
    y_sample = np.concatenate([rs[c]["y_s"] for c in range(NCORE)])[:, None, :].astype(np.float32)
    new_conv_prompt = np.stack([rs[c]["ncp"] for c in range(NCORE)], axis=1).astype(np.float32)
    new_pool_prompt = np.stack([rs[c]["npp"] for c in range(NCORE)], axis=1).astype(np.float32)
    new_conv_sample = np.concatenate([rs[c]["ncs"] for c in range(NCORE)], axis=1).astype(np.float32)
    new_pool_sample = np.concatenate([rs[c]["nps"] for c in range(NCORE)], axis=1).astype(np.float32)
    chunk_v_sample = np.concatenate([rs[c]["cvs"] for c in range(NCORE)], axis=1)[:, :, None, :].astype(np.float32)
    return (y_prompt, y_sample, new_conv_prompt, new_pool_prompt, new_conv_sample, new_pool_sample,
            chunk_v_sample)
```

```python
import numpy as np
from contextlib import ExitStack
import concourse.bass as bass
import concourse.mybir as mybir
from concourse.bass_utils import run_bass_kernel_spmd

F32 = mybir.dt.float32
BF16 = mybir.dt.bfloat16
AF = mybir.ActivationFunctionType
ALU = mybir.AluOpType

D = 1024
DEPTH = 2
NCORE = 8
SEQ = 2048
NSAMP = 16
NPASS = 2
PT = SEQ // NPASS
SP_ = NSAMP
NTOK = PT + 128
NT128 = PT // 128
WT = [(0, 512), (512, 512), (PT, SP_)]
ALPHA = float((2 * DEPTH) ** 0.25)
LN_EPS = 1e-5
POOL_W = (2, 4, 8, 16)
NSLOT = 10
STOP_AFTER = None


class _Stop(Exception):
    pass


def _ck(k):
    if STOP_AFTER is not None and k and k >= STOP_AFTER:
        raise _Stop()
C_U, C_V, C_BG, C_CG, C_XB, C_XC, C_GATE = 0, 1024, 2048, 3072, 4096, 5120, 6144
PV_LNVG, PV_LNVB, PV_CW0, PV_CW1, PV_CW2, PV_CB, PV_PS = range(7)
NPV = 7


class Buf:
    __slots__ = ("name", "lw", "rd")

    def __init__(self, name, init=None):
        self.name = name
        self.lw = dict(init) if init else {}
        self.rd = {}


def _merge(dst, src):
    for k, v in src.items():
        if dst.get(k, -1) < v:
            dst[k] = v


class Op:
    __slots__ = ("emit", "raw", "other", "tok", "dma")


class Sched:
    COMPUTE = ("pe", "act", "dve", "pool")
    NDMASEM = 10

    def __init__(self):
        self.ops = {e: [] for e in ("pe", "act", "dve", "pool", "sp")}
        self.cnt = {e: 0 for e in self.COMPUTE}
        self.dma_n = {"sp": 0, "pool": 0}
        self.dma_cnt = {}

    def add(self, eng, emit, reads=(), writes=(), dma=False):
        op = Op()
        op.emit = emit
        op.dma = dma
        raw, other = {}, {}
        for b in reads:
            _merge(raw, b.lw)
        for b in writes:
            _merge(other, b.lw)
            _merge(other, b.rd)
        if dma:
            i = self.dma_n[eng] % self.NDMASEM
            self.dma_n[eng] += 1
            key = ("dma", eng, i)
            n = self.dma_cnt.get(key, 0)
            if n > 0:
                _merge(other, {key: 16 * n})
            self.dma_cnt[key] = n + 1
            tok = (key, 16 * (n + 1))
        else:
            self.cnt[eng] += 1
            tok = (eng, self.cnt[eng])
        op.raw, op.other, op.tok = raw, other, tok
        t = {tok[0]: tok[1]}
        for b in writes:
            b.lw = dict(t)
            b.rd = {}
        for b in reads:
            _merge(b.rd, t)
        self.ops[eng].append(op)
        return tok

    @staticmethod
    def retag(bufs):
        d = {}
        for b in bufs:
            _merge(d, b.lw)
            _merge(d, b.rd)
        return d

    def emit_all(self, nc, block, sems, final_waits):
        sched = self

        def run(engname, e):
            known = {}
            for op in sched.ops[engname]:
                deps = {}
                for k, v in op.raw.items():
                    if k == engname and engname == "pe":
                        continue
                    deps[k] = max(deps.get(k, -1), v)
                for k, v in op.other.items():
                    if k == engname:
                        continue
                    deps[k] = max(deps.get(k, -1), v)
                for k, v in deps.items():
                    if known.get(k, -1) >= v:
                        continue
                    e.wait_ge(sems[k], v)
                    known[k] = v
                ins = op.emit(e)
                if op.dma:
                    ins.then_inc(sems[op.tok[0]], 16)
                else:
                    ins.then_inc(sems[op.tok[0]], 1)
            if engname == "sp":
                for k, v in final_waits.items():
                    if known.get(k, -1) < v:
                        e.wait_ge(sems[k], v)

        @block.tensor
        def _(e):
            run("pe", e)

        @block.scalar
        def _(e):
            run("act", e)

        @block.vector
        def _(e):
            run("dve", e)

        @block.gpsimd
        def _(e):
            run("pool", e)

        @block.sync
        def _(e):
            run("sp", e)


def _host_consts():
    ident = np.eye(128, dtype=np.float32)
    s = np.arange(128)[:, None]
    t = np.arange(128)[None, :]
    mask = (s <= t).astype(np.float32)
    bands = np.zeros((128, 12, 128), np.float32)
    for g, w in enumerate(POOL_W):
        main = ((s <= t) & (s > t - w)).astype(np.float32) / w - (s == t).astype(np.float32)
        cnt = np.minimum(t + 1, w).astype(np.float32)
        first = ((s <= t) & (s > t - w)).astype(np.float32) / cnt - (s == t).astype(np.float32)
        halo = ((s - 128) > (t - w)).astype(np.float32) / w
        bands[:, g * 3 + 0, :] = main
        bands[:, g * 3 + 1, :] = first
        bands[:, g * 3 + 2, :] = halo
    sel = np.zeros((128, 2, 4, SP_), np.float32)
    i8w = np.zeros((128, 4, SP_), np.float32)
    for g, w in enumerate(POOL_W):
        for b in range(SP_):
            for r in range(15):
                if r >= 15 - (w - 1):
                    sel[(b % 8) * 15 + r, b // 8, g, b] = 1.0 / w
            i8w[b, g, b] = 1.0 / w - 1.0
    return ident, mask, bands, sel, i8w


def build_nc():
    nc = bass.Bass("TRN2", target_bir_lowering=False)
    es = ExitStack()

    def din(name, shape):
        return nc.dram_tensor(name, list(shape), F32, kind="ExternalInput").ap()

    def dout(name, shape):
        return nc.dram_tensor(name, list(shape), F32, kind="ExternalOutput").ap()

    x_p = din("x_p", [SEQ, D])
    x_s = din("x_s", [NSAMP, D])
    sconv = din("sconv", [DEPTH, NSAMP, 2, D])
    spool = din("spool", [DEPTH, NSAMP, 15, D])
    w_in = din("w_in", [DEPTH, D, 9216])
    w_pa = din("w_proj_a", [DEPTH, D, D])
    w_pb = din("w_proj_b", [DEPTH, D, D])
    w_pc = din("w_proj_c", [DEPTH, D, D])
    w_o = din("w_o", [DEPTH, D, D])
    w_f1 = din("w_ff1", [DEPTH, D, 4 * D])
    w_f2 = din("w_ff2", [DEPTH, 4 * D, D])
    w_pool = din("w_pool", [DEPTH, 4, 256, 256])
    pvec_d = din("pvec", [128, DEPTH * NPV * 8])
    lnbc_d = din("lnbc", [DEPTH, 4, 128, D])
    wsT_d = din("wsT", [DEPTH, 128, 4 * 128])
    bsbc_d = din("bsbc", [DEPTH, 128, 4 * 128])
    ws00_d = din("ws00", [DEPTH, 128, 4])
    bs0_d = din("bs0", [DEPTH, 128, 4])
    c_ident = din("c_ident", [128, 128])
    c_mask = din("c_mask", [128, 128])
    c_bands = din("c_bands", [128, 12 * 128])
    c_sel = din("c_sel", [128, 2 * 4 * SP_])
    c_i8w = din("c_i8w", [128, 4 * SP_])

    y_p = dout("y_p", [SEQ, D])
    y_s = dout("y_s", [NSAMP, D])
    ncp = dout("ncp", [DEPTH, 2, D])
    npp = dout("npp", [DEPTH, 15, D])
    ncs = dout("ncs", [DEPTH, NSAMP, 2, D])
    nps = dout("nps", [DEPTH, NSAMP, 15, D])
    cvs = dout("cvs", [DEPTH, NSAMP, D])

    def sb(name, shape, dt=F32):
        return es.enter_context(nc.sbuf_tensor(name, list(shape), dt))

    R = sb("R", [128, 9, D])
    T = sb("T", [128, 8, NTOK], BF16)
    B1 = sb("B1", [128, 9 * D], BF16)
    B2 = sb("B2", [128, 8 * NTOK], BF16)
    ring = sb("ring", [128, NSLOT, 8, 256], BF16)
    lnbc = sb("lnbc_sb", [128, 4, D])
    Dt = sb("Dt", [128, 2, 8, 128], BF16)
    NFT = 6
    ftmp = sb("ftmp", [128, NFT, 512])
    zbuf = sb("zbuf", [128, 3, 514], BF16)
    zsb = sb("zsb", [128, 8, SP_, 3], BF16)
    xctm = sb("xctm", [128, 3, D], BF16)
    xccar = sb("xccar", [128, DEPTH, D], BF16)
    zcar = sb("zcar", [128, DEPTH, 8, 2], BF16)
    stage = sb("stage", [128, 2, D])
    ident = sb("ident", [128, 128])
    ones = sb("ones", [128, 128])
    maskt = sb("maskt", [128, 128])
    pvec = sb("pvec_sb", [128, DEPTH * NPV * 8])
    bands = sb("bands", [128, 12, 128], BF16)
    selb = sb("selb", [128, 2, 4, SP_], BF16)
    i8wb = sb("i8wb", [128, 4, SP_], BF16)
    WsT = sb("WsT", [128, 4, 128], BF16)
    WsS = sb("WsS", [128, 4, SP_], BF16)
    diag = sb("diag", [128, 8, 3, 128], BF16)
    Bias2 = sb("Bias2", [128, 8, 128])
    Bias2s = sb("Bias2s", [128, 8])
    ws00 = sb("ws00_sb", [128, 4])
    bs0 = sb("bs0_sb", [128, 4])
    zsf = sb("zsf", [128, 8, 128])
    vnT = zsf
    NSM = 8
    stat = sb("stat", [128, NSM, 12])
    mv = sb("mv", [128, NSM, 2])
    sm = sb("sm", [128, NSM, 4])
    epsb = sb("epsb", [128, 1])

    ps = es.enter_context(nc.psum_tensor("ps", [128, 8 * 512], F32))

    S = Sched()

    bank = [Buf(f"bank{i}") for i in range(8)]
    slotb = [Buf(f"slot{i}") for i in range(NSLOT)]
    Rb = [Buf(f"R{i}") for i in range(9)]
    Tb = [Buf(f"T{i}") for i in range(9)]
    ftb = [Buf(f"ft{i}") for i in range(NFT)]
    zbb = [Buf("zb0"), Buf("zb1"), Buf("zb2")]
    zsbb = Buf("zsb")
    xcb = [Buf("xc0"), Buf("xc1"), Buf("xc2")]
    xccb = [Buf("xcc0"), Buf("xcc1")]
    zcb = [Buf("zc0"), Buf("zc1")]
    ppb = Buf("pp")
    stg = [Buf("stg0"), Buf("stg1")]
    Dtb = [Buf("Dt0"), Buf("Dt1")]
    lnb = Buf("lnbc")
    cst = Buf("consts")
    cst2 = Buf("consts_pool")
    cst3 = Buf("consts_dve")
    lcst = Buf("layer_consts")
    smallb = [Buf(f"small{i}") for i in range(8)]
    zsfb = Buf("zsf")
    vnb = zsfb
    state = {"bank1": 0, "bank2": 0, "slot": 0, "ft": 0, "stg": 0, "small": 0, "zb": 0, "xc": 0, "dt": 0}

    def bank1():
        i = state["bank1"] % 8
        state["bank1"] += 1
        return i

    def bank2():
        i = state["bank2"] % 4
        state["bank2"] += 1
        return 2 * i

    def pcols(b, a, n):
        return ps[:, b * 512 + a: b * 512 + a + n]

    def ft_next():
        i = state["ft"] % NFT
        state["ft"] += 1
        return i

    def stg_next():
        i = state["stg"] % 2
        state["stg"] += 1
        return i

    def small_next():
        i = state["small"] % 8
        state["small"] += 1
        return i

    def dma_sp(out, in_, reads=(), writes=()):
        return S.add("sp", lambda e: e.dma_start(out=out, in_=in_), reads, writes, dma=True)

    def dma_pool(out, in_, reads=(), writes=()):
        return S.add("pool", lambda e: e.dma_start(out=out, in_=in_), reads, writes, dma=True)

    def load_unit(src2d):
        s = state["slot"] % NSLOT
        state["slot"] += 1
        dma_pool(ring[:, s, :, :], src2d.rearrange("(kc p) n -> p kc n", p=128), writes=[slotb[s]])
        return s

    def load_dunit(src2d):
        if state["slot"] % 2 == 1:
            state["slot"] += 1
        s = state["slot"] % NSLOT
        state["slot"] += 2
        view = ring[:, s:s + 2, :, :].rearrange("p a k n -> p (a k n)").rearrange("p (k n) -> p k n", k=8)
        dma_pool(view, src2d.rearrange("(kc p) n -> p kc n", p=128), writes=[slotb[s], slotb[s + 1]])
        return (s, view)

    def pe_group(mms, reads, writes):
        def emit(e):
            ins = None
            for (o, l, r, st, sp_) in mms:
                ins = e.matmul(o, l, r, start=st, stop=sp_)
            return ins
        return S.add("pe", emit, reads, writes)

    def pe_transposes(items, reads, writes):
        def emit(e):
            ins = None
            for (o, i, idn) in items:
                ins = e.transpose(o, i, idn)
            return ins
        return S.add("pe", emit, reads, writes)

    def act(out, in_, func, reads, writes, bias=None, scale=None, accum_out=None):
        kw = {}
        if bias is not None:
            kw["bias"] = bias
        if scale is not None:
            kw["scale"] = scale
        if accum_out is not None:
            kw["accum_out"] = accum_out
        return S.add("act", lambda e: e.activation(out=out, in_=in_, func=func, **kw), reads, writes)

    def dve_tt(out, in0, in1, op, reads, writes):
        return S.add("dve", lambda e: e.tensor_tensor(out=out, in0=in0, in1=in1, op=op), reads, writes)

    def dve_ts(out, in0, s1, s2, op0, op1, reads, writes):
        if op1 is None:
            return S.add("dve", lambda e: e.tensor_scalar(out=out, in0=in0, scalar1=s1, scalar2=None, op0=op0),
                         reads, writes)
        return S.add("dve", lambda e: e.tensor_scalar(out=out, in0=in0, scalar1=s1, scalar2=s2, op0=op0, op1=op1),
                     reads, writes)

    def dve_stt(out, in0, scalar, in1, op0, op1, reads, writes, accum_out=None):
        if accum_out is not None:
            return S.add("dve", lambda e: e.scalar_tensor_tensor(out=out, in0=in0, scalar=scalar, in1=in1,
                                                                 op0=op0, op1=op1, accum_out=accum_out),
                         reads, writes)
        return S.add("dve", lambda e: e.scalar_tensor_tensor(out=out, in0=in0, scalar=scalar, in1=in1,
                                                             op0=op0, op1=op1), reads, writes)

    def dve_copy(out, in_, reads, writes):
        return S.add("dve", lambda e: e.tensor_copy(out=out, in_=in_), reads, writes)

    def pv(l, v, c):
        i = (l * NPV + v) * 8 + c
        return pvec[:, i:i + 1]

    dma_sp(ident[:, :], c_ident, writes=[cst])
    dma_sp(maskt[:, :], c_mask, writes=[cst])
    dma_sp(pvec[:, :], pvec_d, writes=[cst])
    def load_pool_consts():
        dma_pool(bands[:, :, :], c_bands.rearrange("p (a b) -> p a b", a=12), writes=[cst2])
        dma_pool(selb[:, :, :, :], c_sel.rearrange("p (h a b) -> p h a b", h=2, a=4), writes=[cst2])
        dma_pool(i8wb[:, :, :], c_i8w.rearrange("p (a b) -> p a b", a=4), writes=[cst2])
    S.add("dve", lambda e: e.memset(ones[:, :], 1.0), (), [cst3])
    S.add("dve", lambda e: e.memset(epsb[:, :], LN_EPS), (), [cst3])
    S.add("dve", lambda e: e.memset(zsf[:, :, :], 0.0), (), [zsfb])

    out_tokens = {}

    def passthrough_copies():
        for l_ in range(DEPTH):
            tk = dma_sp(ncs[l_, :, 0, :], sconv[l_, :, 1, :])
            out_tokens[tk[0]] = max(out_tokens.get(tk[0], 0), tk[1])
            for h in range(2):
                tk = dma_sp(nps[l_, h * 8:(h + 1) * 8, 0:14, :], spool[l_, h * 8:(h + 1) * 8, 1:15, :])
                out_tokens[tk[0]] = max(out_tokens.get(tk[0], 0), tk[1])

    def out_dma(out, in_, reads):
        tk = dma_sp(out, in_, reads=reads)
        out_tokens[tk[0]] = max(out_tokens.get(tk[0], 0), tk[1])

    def tile_rows(i):
        return 128

    def tile_cols(i):
        return (i * 128, 128) if i < NT128 else (PT, SP_)

    def wt_tiles(w):
        return list(range(4 * w, 4 * w + 4)) if w < 2 else [8]

    def transpose_R_to_T(i):
        rows = tile_rows(i)
        c0, n = tile_cols(i)
        n = 128
        b = bank2()
        items = []
        for c in range(8):
            items.append((ps[:, b * 512 + c * 128: b * 512 + c * 128 + rows],
                          R[0:rows, i, c * 128:(c + 1) * 128], ident[0:rows, 0:rows]))
        pe_transposes(items, [Rb[i], cst], [bank[b], bank[b + 1]])
        src = ps[:, b * 512:(b + 2) * 512].rearrange("p (c t) -> p c t", c=8)[:, :, 0:rows]
        act(T[:, :, c0:c0 + n], src, AF.Copy, [bank[b], bank[b + 1]], [Tb[i]])

    def xstat_project(i, slots, dst_reads, lhs_of_kc):
        rows = tile_rows(i)
        b = bank2()
        mms = []
        for h in range(2):
            for kc in range(8):
                mms.append((ps[0:rows, (b + h) * 512:(b + h + 1) * 512],
                            lhs_of_kc(kc), slots[h][1][:, kc, :], kc == 0, kc == 7))
        sl = []
        for (s0, _v) in slots:
            sl += [slotb[s0], slotb[s0 + 1]]
        pe_group(mms, dst_reads + sl, [bank[b], bank[b + 1]])
        return b

    def run_pipeline(n_items, stages, lags):
        for t in range(n_items + max(lags)):
            for f, lag in zip(stages, lags):
                i = t - lag
                if 0 <= i < n_items:
                    f(i)

    def ln_pipeline(pre, gi, post, extra=()):
        st = {}

        def s_stats(i):
            k, sbuf_ = st[i]
            dve_tt(stat[:, k, 2:3], stat[:, k, 0:1], stat[:, k, 0:1], ALU.mult, [sbuf_], [sbuf_])
            dve_stt(mv[:, k, 1:2], stat[:, k, 1:2], float(D), stat[:, k, 2:3], ALU.mult, ALU.subtract,
                    [sbuf_], [sbuf_])
            act(sm[:, k, 0:1], mv[:, k, 1:2], AF.Sqrt, [sbuf_, cst3], [sbuf_], bias=epsb[:, :],
                scale=1.0 / (float(D) * float(D)))

        def s_rstd(i):
            k, sbuf_ = st[i]
            S.add("dve", lambda e: e.reciprocal(out=sm[:, k, 1:2], in_=sm[:, k, 0:1]), [sbuf_], [sbuf_])
            dve_stt(sm[:, k, 2:3], stat[:, k, 0:1], -1.0 / D, sm[:, k, 1:2], ALU.mult, ALU.mult, [sbuf_], [sbuf_])
            act(R[:, i, :], R[:, i, :], AF.Identity, [Rb[i], sbuf_], [Rb[i]], bias=sm[:, k, 2:3], scale=sm[:, k, 1:2])

        def s_affine(i):
            dve_tt(R[:, i, :], R[:, i, :], lnbc[:, gi, :], ALU.mult, [Rb[i], lnb], [Rb[i]])
            dve_tt(R[:, i, :], R[:, i, :], lnbc[:, gi + 1, :], ALU.add, [Rb[i], lnb], [Rb[i]])

        def s_pre(i):
            k = small_next()
            sbuf_ = smallb[k]
            st[i] = (k, sbuf_)
            pre(i, stat[:, k, 0:1], sbuf_)
            sj = stg_next()
            act(stage[:, sj, :], R[:, i, :], AF.Square, [Rb[i]], [stg[sj], sbuf_], accum_out=stat[:, k, 1:2])

        run_pipeline(cur["NTI"], [s_pre, s_affine, s_stats, s_rstd] + [f for f, _l in extra] + [post],
                     [0, 2, 0, 1] + [lg for _f, lg in extra] + [3])

    def merge_branch(l, br, hbufs, wproj, first, Mb_new):
        hv = B2[:, :].rearrange("p (c t) -> p c t", c=8)
        mg = B1[:, 0:8 * NTOK].rearrange("p (c t) -> p c t", c=8)
        items = []
        units = {}

        def g_stage(k):
            jp, jj, w, t0, n = items[k]
            if jp not in units:
                units[jp] = (load_unit(wproj[l, :, jp * 256:(jp + 1) * 256]),
                             load_unit(w_in[l, :, C_GATE + br * D + jp * 256: C_GATE + br * D + (jp + 1) * 256]))
            sg = units[jp][1]
            bg_ = bank1()
            pe_group([(pcols(bg_, 0, n), ring[:, sg, kc, jj * 128:(jj + 1) * 128], T[:, kc, t0:t0 + n],
                       kc == 0, kc == 7) for kc in range(8)],
                     [slotb[sg]] + [Tb[i] for i in wt_tiles(w)], [bank[bg_]])
            f = ft_next()
            act(ftmp[:, f, 0:n], pcols(bg_, 0, n), AF.Sigmoid, [bank[bg_]], [ftb[f]])
            items[k] = (jp, jj, w, t0, n, f)

        def p_stage(k):
            jp, jj, w, t0, n, f = items[k]
            j = jp * 2 + jj
            sp_ = units[jp][0]
            bp = bank1()
            pe_group([(pcols(bp, 0, n), ring[:, sp_, kc, jj * 128:(jj + 1) * 128], hv[:, kc, t0:t0 + n],
                       kc == 0, kc == 7) for kc in range(8)],
                     [slotb[sp_]] + [hbufs[(kc, w)] for kc in range(8)], [bank[bp]])
            mb = Mb_new[(j, w)]
            if first:
                dve_tt(mg[:, j, t0:t0 + n], ftmp[:, f, 0:n], pcols(bp, 0, n), ALU.mult,
                       [ftb[f], bank[bp]], [mb])
            else:
                f2 = ft_next()
                dve_tt(ftmp[:, f2, 0:n], ftmp[:, f, 0:n], pcols(bp, 0, n), ALU.mult,
                       [ftb[f], bank[bp]], [ftb[f2]])
                dve_tt(mg[:, j, t0:t0 + n], mg[:, j, t0:t0 + n], ftmp[:, f2, 0:n], ALU.add,
                       [ftb[f2], mb], [mb])

        for jp in range(4):
            for jj in range(2):
                for w, (t0, n) in enumerate(cur["WT"]):
                    items.append((jp, jj, w, t0, n))
        run_pipeline(len(items), [g_stage, p_stage], [0, 2])

    def build_layer_consts(l2, has_s2):
        si = stg_next()
        dma_sp(stage[:, si, 0:512], wsT_d[l2, :, :], writes=[stg[si]])
        si2 = stg_next()
        dma_sp(stage[:, si2, 0:512], bsbc_d[l2, :, :], writes=[stg[si2]])
        dma_sp(ws00[:, :], ws00_d[l2, :, :], writes=[lcst])
        dma_sp(bs0[:, :], bs0_d[l2, :, :], writes=[lcst])
        st3 = stage[:, si, 0:512].rearrange("p (g t) -> p g t", g=4)
        for g in range(4):
            dve_tt(st3[:, g, :], st3[:, g, :], maskt[:, :], ALU.mult, [stg[si], cst], [stg[si]])
        dve_copy(WsT[:, :, :], st3, [stg[si]], [lcst])
        br_ = bank1()
        pe_group([(pcols(br_, 0, 512), ones[:, :], stage[:, si, 0:512], True, True)],
                 [stg[si], cst3], [bank[br_]])
        bs3 = stage[:, si2, 0:512].rearrange("p (g t) -> p g t", g=4)
        for c in range(8):
            g = c // 2
            dve_stt(Bias2[:, c, :], pcols(br_, g * 128, 128), pv(l2, PV_LNVB, c), bs3[:, g, :],
                    ALU.mult, ALU.add, [bank[br_], stg[si2], cst], [lcst])
            dve_stt(Bias2s[:, c:c + 1], ws00[:, g:g + 1], pv(l2, PV_LNVB, c), bs0[:, g:g + 1],
                    ALU.mult, ALU.add, [lcst, cst], [lcst])
            for k3 in range(3):
                dve_ts(diag[:, c, k3, :], ident[:, :], pv(l2, PV_CW0 + k3, c), None, ALU.mult, None,
                       [cst], [lcst])
        for g in range(4):
            dve_ts(WsS[:, g, :], ident[:, 0:SP_], ws00[:, g:g + 1], None, ALU.mult, None,
                   [cst, lcst], [lcst])
        if has_s2:
            si = stg_next()
            dma_sp(stage[0:2 * SP_, si, :],
                   sconv[l2, 0:SP_, :, :].rearrange("b k d -> (b k) d"), writes=[stg[si]])
            bz = bank2()
            pe_transposes([(pcols(bz, c * 128, 128), stage[:, si, c * 128:(c + 1) * 128], ident[:, :])
                           for c in range(8)], [stg[si], cst], [bank[bz], bank[bz + 1]])
            for c in range(8):
                dve_copy(zsb[:, c, :, 0:2], pcols(bz, c * 128, 2 * SP_).rearrange("p (b k) -> p b k", k=2),
                         [bank[bz], bank[bz + 1]], [zsbb])

    cur = {"NTI": 9, "WT": WT}
    B1all = [Buf("B1init")]
    B2all = [Buf("B2init")]

    try:
      _ck(0)
      for p in range(NPASS):
          tok_base = p * PT
          smp_base = 0
          has_s = (p == 0)
          NTI = 9 if has_s else 8
          WTp = WT if has_s else WT[:2]
          cur["NTI"], cur["WT"] = NTI, WTp
          if p == 0:
              for i in range(NT128):
                  dma_sp(R[:, i, :], x_p[tok_base + i * 128: tok_base + (i + 1) * 128, :], writes=[Rb[i]])
          if has_s:
              S.add("dve", lambda e: e.memset(R[:, 8, :], 0.0), (), [Rb[8]])
              dma_sp(R[0:SP_, 8, :], x_s[smp_base:smp_base + SP_, :], writes=[Rb[8]])
          _ck(0.5)
          for i in range(NTI):
              transpose_R_to_T(i)
              _ck(0.6 + 0.01 * i)

          _ck(1)
          for l in range(DEPTH):
              kstep = p * DEPTH + l

              _ck(2)
              dA = Sched.retag(B1all)
              XH = [Buf(f"xh{i}", dA) for i in range(9)]
              B1all = XH
              xh = B1[:, :].rearrange("p (i d) -> p i d", i=9)
              slots = [load_dunit(w_in[l, :, C_V + h * 512: C_V + (h + 1) * 512]) for h in range(2)]
              _ck(2.1)
              stA = {}

              def a1_proj(i, slots=slots):
                  c0, n = tile_cols(i)
                  b = xstat_project(i, slots, [Tb[i]], lambda kc, c0=c0: T[:, kc, c0:c0 + 128])
                  k = small_next()
                  sbuf_ = smallb[k]
                  stA[i] = (b, k, sbuf_)
                  a0, a1 = ps[:, b * 512:(b + 1) * 512], ps[:, (b + 1) * 512:(b + 2) * 512]
                  S.add("dve", lambda e: e.bn_stats(out=stat[:, k, 0:6], in_=a0), [bank[b]], [sbuf_])
                  S.add("dve", lambda e: e.bn_stats(out=stat[:, k, 6:12], in_=a1), [bank[b + 1]], [sbuf_])
                  S.add("dve", lambda e: e.bn_aggr(out=mv[:, k, :], in_=stat[:, k, :]), [sbuf_], [sbuf_])
                  act(sm[:, k, 0:1], mv[:, k, 1:2], AF.Sqrt, [sbuf_, cst3], [sbuf_], bias=epsb[:, :], scale=1.0)

              def a1_norm(i):
                  b, k, sbuf_ = stA[i]
                  S.add("dve", lambda e: e.reciprocal(out=sm[:, k, 1:2], in_=sm[:, k, 0:1]), [sbuf_], [sbuf_])
                  dve_stt(sm[:, k, 2:3], mv[:, k, 0:1], -1.0, sm[:, k, 1:2], ALU.mult, ALU.mult, [sbuf_], [sbuf_])
                  act(xh[:, i, :], ps[:, b * 512:(b + 2) * 512], AF.Identity,
                      [bank[b], bank[b + 1], sbuf_], [XH[i]], bias=sm[:, k, 2:3], scale=sm[:, k, 1:2])
                  if i == 8:
                      sx = stg_next()
                      stA["xhs"] = sx
                      act(stage[:, sx, :], ps[:, b * 512:(b + 2) * 512], AF.Identity,
                          [bank[b], bank[b + 1], sbuf_], [stg[sx]], bias=sm[:, k, 2:3], scale=sm[:, k, 1:2])

              run_pipeline(NTI, [a1_proj, a1_norm], [0, 1])
              _ck(3)
              if has_s:
                  bt = bank2()
                  sx = stA["xhs"]
                  pe_transposes([(pcols(bt, c * 128, 128), stage[:, sx, c * 128:(c + 1) * 128], ident[:, :])
                                 for c in range(8)], [stg[sx], cst], [bank[bt], bank[bt + 1]])
                  for c in range(8):
                      act(vnT[:, c, 0:SP_], pcols(bt, c * 128, SP_), AF.Identity, [bank[bt], bank[bt + 1], cst], [vnb],
                          bias=pv(l, PV_LNVB, c), scale=pv(l, PV_LNVG, c))
                  bt2 = bank2()
                  pe_transposes([(ps[:, bt2 * 512 + c * 128: bt2 * 512 + (c + 1) * 128], vnT[:, c, :], ident[:, :])
                                 for c in range(8)], [vnb, cst], [bank[bt2], bank[bt2 + 1]])
                  si = stg_next()
                  act(stage[0:SP_, si, :], ps[0:SP_, bt2 * 512:(bt2 + 2) * 512], AF.Copy,
                      [bank[bt2], bank[bt2 + 1]], [stg[si]])
                  out_dma(cvs[l, smp_base:smp_base + SP_, :], stage[0:SP_, si, :], [stg[si]])

              if kstep == 0:
                  build_layer_consts(l, has_s)
              for v in range(4):
                  dma_sp(lnbc[:, v, :], lnbc_d[l, v, :, :], writes=[lnb])
              if kstep == 0:
                  passthrough_copies()
                  load_pool_consts()

              _ck(4)
              dB2 = Sched.retag(B2all)
              HA = {(c, w): Buf(f"ha{c}_{w}", dB2) for c in range(8) for w in range(3)}
              B2all = list(HA.values())
              hv = B2[:, :].rearrange("p (c t) -> p c t", c=8)
              for cp in range(4):
                  su = load_unit(w_in[l, :, C_U + cp * 256: C_U + (cp + 1) * 256])
                  for cc in range(2):
                      c = cp * 2 + cc
                      g = c // 2
                      for w, (t0, n) in enumerate(cur["WT"]):
                          bu = bank1()
                          pe_group([(pcols(bu, 0, n), ring[:, su, kc, cc * 128:(cc + 1) * 128], T[:, kc, t0:t0 + n],
                                     kc == 0, kc == 7) for kc in range(8)],
                                   [slotb[su]] + [Tb[i] for i in wt_tiles(w)], [bank[bu]])
                          bs_ = bank1()
                          f = ft_next()
                          if w < 2:
                              pe_group([(pcols(bs_, j * 128, 128), xh[:, 4 * w + j, c * 128:(c + 1) * 128],
                                         WsT[:, g, :], True, True) for j in range(4)],
                                       [XH[4 * w + j] for j in range(4)] + [lcst], [bank[bs_]])
                              dve_stt(ftmp[:, f, :].rearrange("p (j t) -> p j t", j=4),
                                      pcols(bs_, 0, 512).rearrange("p (j t) -> p j t", j=4),
                                      pv(l, PV_LNVG, c),
                                      Bias2[:, c:c + 1, :].to_broadcast([128, 4, 128]),
                                      ALU.mult, ALU.add, [bank[bs_], lcst, cst], [ftb[f]])
                          else:
                              pe_group([(pcols(bs_, 0, SP_), xh[:, 8, c * 128:(c + 1) * 128],
                                         WsS[:, g, :], True, True)], [XH[8], lcst], [bank[bs_]])
                              dve_stt(ftmp[:, f, 0:SP_], pcols(bs_, 0, SP_), pv(l, PV_LNVG, c),
                                      Bias2s[:, c:c + 1].to_broadcast([128, SP_]),
                                      ALU.mult, ALU.add, [bank[bs_], lcst, cst], [ftb[f]])
                          dve_tt(hv[:, c, t0:t0 + n], ftmp[:, f, 0:n], pcols(bu, 0, n), ALU.mult,
                                 [ftb[f], bank[bu]], [HA[(c, w)]])

              _ck(5)
              dM = Sched.retag(B1all)
              MG = {(j, w): Buf(f"mg{j}_{w}", dM) for j in range(8) for w in range(3)}
              B1all = list(MG.values())
              merge_branch(l, 0, HA, w_pa, True, MG)

              _ck(6)
              dB2 = Sched.retag(B2all)
              HB = {(c, w): Buf(f"hb{c}_{w}", dB2) for c in range(8) for w in range(3)}
              B2all = list(HB.values())
              pendB = []

              def b_conv(item):
                  (c, w, t0, n, zsrc, zbufs, by, fbg) = item
                  pe_group([(pcols(by, 0, n), diag[:, c, k3, :], zsrc(k3), k3 == 0, k3 == 2)
                            for k3 in range(3)], zbufs + [lcst], [bank[by]])
                  dve_stt(hv[:, c, t0:t0 + n], pcols(by, 0, n), pv(l, PV_CB, c), ftmp[:, fbg, 0:n],
                          ALU.add, ALU.mult, [bank[by], ftb[fbg], cst], [HB[(c, w)]])

              for cp in range(4):
                  sbg = load_unit(w_in[l, :, C_BG + cp * 256: C_BG + (cp + 1) * 256])
                  scg = load_unit(w_in[l, :, C_CG + cp * 256: C_CG + (cp + 1) * 256])
                  sxb = load_unit(w_in[l, :, C_XB + cp * 256: C_XB + (cp + 1) * 256])
                  for cc in range(2):
                      c = cp * 2 + cc
                      prev_z = None
                      for w, (t0, n) in enumerate(cur["WT"]):
                          trd = [Tb[i] for i in wt_tiles(w)]
                          bks = []
                          for su_ in (sbg, scg, sxb):
                              bb = bank1()
                              pe_group([(pcols(bb, 0, n), ring[:, su_, kc, cc * 128:(cc + 1) * 128],
                                         T[:, kc, t0:t0 + n], kc == 0, kc == 7) for kc in range(8)],
                                       [slotb[su_]] + trd, [bank[bb]])
                              bks.append(bb)
                          bbg, bcg, bxb = bks
                          fcg = ft_next()
                          act(ftmp[:, fcg, 0:n], pcols(bcg, 0, n), AF.Copy, [bank[bcg]], [ftb[fcg]])
                          fbg = ft_next()
                          act(ftmp[:, fbg, 0:n], pcols(bbg, 0, n), AF.Copy, [bank[bbg]], [ftb[fbg]])
                          by = bank1()
                          if w < 2:
                              zi = state["zb"] % 3
                              state["zb"] += 1
                              dve_tt(zbuf[:, zi, 2:2 + n], ftmp[:, fcg, 0:n], pcols(bxb, 0, n), ALU.mult,
                                     [ftb[fcg], bank[bxb]], [zbb[zi]])
                              if w == 0:
                                  if p == 0:
                                      S.add("dve", lambda e, zi=zi: e.memset(zbuf[:, zi, 0:2], 0.0), (), [zbb[zi]])
                                  else:
                                      dve_copy(zbuf[:, zi, 0:2], zcar[:, l, c, :], [zcb[l]], [zbb[zi]])
                              else:
                                  dve_copy(zbuf[:, zi, 0:2], zbuf[:, prev_z, 512:514], [zbb[prev_z]], [zbb[zi]])
                                  if p == 0:
                                      dve_copy(zcar[:, l, c, :], zbuf[:, zi, 512:514], [zbb[zi]], [zcb[l]])
                                  else:
                                      dve_tt(zsf[:, c, SP_:SP_ + 2], ftmp[:, fcg, n - 2:n], pcols(bxb, n - 2, 2),
                                             ALU.mult, [ftb[fcg], bank[bxb]], [zsfb])
                              prev_z = zi
                              item = (c, w, t0, n, (lambda k3, zi=zi, n=n: zbuf[:, zi, k3:k3 + n]), [zbb[zi]], by, fbg)
                          else:
                              dve_tt(zsb[:, c, :, 2], ftmp[:, fcg, 0:n], pcols(bxb, 0, n), ALU.mult,
                                     [ftb[fcg], bank[bxb]], [zsbb])
                              dve_tt(zsf[:, c, 0:SP_], ftmp[:, fcg, 0:n], pcols(bxb, 0, n), ALU.mult,
                                     [ftb[fcg], bank[bxb]], [zsfb])
                              item = (c, w, t0, n, (lambda k3, c=c: zsb[:, c, :, k3]), [zsbb], by, fbg)
                          pendB.append(item)
                          if len(pendB) > 1:
                              b_conv(pendB.pop(0))
              while pendB:
                  b_conv(pendB.pop(0))
              if has_s or p == NPASS - 1:
                  bt = bank2()
                  pe_transposes([(ps[:, bt * 512 + c * 128: bt * 512 + (c + 1) * 128], zsf[:, c, :], ident[:, :])
                                 for c in range(8)], [zsfb, cst], [bank[bt], bank[bt + 1]])
                  si = stg_next()
                  act(stage[0:SP_ + 2, si, :], ps[0:SP_ + 2, bt * 512:(bt + 2) * 512], AF.Copy,
                      [bank[bt], bank[bt + 1]], [stg[si]])
                  if has_s:
                      out_dma(ncs[l, smp_base:smp_base + SP_, 1, :], stage[0:SP_, si, :], [stg[si]])
                  if p == NPASS - 1:
                      out_dma(ncp[l, :, :], stage[SP_:SP_ + 2, si, :], [stg[si]])

              _ck(7)
              merge_branch(l, 1, HB, w_pb, False, MG)

              if kstep + 1 < NPASS * DEPTH:
                  p2_, l2_ = divmod(kstep + 1, DEPTH)
                  build_layer_consts(l2_, p2_ == 0)

              _ck(8)
              dB2 = Sched.retag(B2all)
              HC = {(c, w): Buf(f"hc{c}_{w}", dB2) for c in range(8) for w in range(3)}
              B2all = list(HC.values())
              slots = [load_dunit(w_in[l, :, C_XC + h * 512: C_XC + (h + 1) * 512]) for h in range(2)]
              swp = load_unit(w_pool[l].rearrange("g k d -> (g k) d"))
              spp, ppv = None, None
              if has_s:
                  spp = state["slot"] % NSLOT
                  state["slot"] += 1
                  ppv = ring[:, spp, :, :].rearrange("p k n -> p (k n)").rearrange("p (h d) -> p h d", h=2)
                  for h in range(2):
                      dma_pool(ppv[0:120, h, :],
                               spool[l, h * 8:(h + 1) * 8, :, :].rearrange("b r d -> (b r) d"), writes=[slotb[spp]])
              _ck(8.1)
              stC = {}

              def c_proj(i, slots=slots, l=l, p=p, smp_base=smp_base):
                  c0, n = tile_cols(i)
                  b = xstat_project(i, slots, [Tb[i]], lambda kc, c0=c0: T[:, kc, c0:c0 + 128])
                  if i == NT128 - 1:
                      cur_ap, cur_b = xccar[:, l, :], xccb[l]
                  else:
                      xi = state["xc"] % 3
                      state["xc"] += 1
                      cur_ap, cur_b = xctm[:, xi, :], xcb[xi]
                  if i == 0:
                      prev = (xccar[:, l, :], xccb[l]) if p > 0 else None
                  else:
                      prev = stC[i - 1][0:2]
                  stC[i] = (cur_ap, cur_b, prev)
                  act(cur_ap, ps[:, b * 512:(b + 2) * 512], AF.Copy, [bank[b], bank[b + 1]], [cur_b])
                  if (i == NT128 - 1 and p == NPASS - 1) or i == 8:
                      si = stg_next()
                      act(stage[:, si, :], ps[:, b * 512:(b + 2) * 512], AF.Copy, [bank[b], bank[b + 1]], [stg[si]])
                      if i == 8:
                          out_dma(nps[l, smp_base:smp_base + SP_, 14, :], stage[0:SP_, si, :], [stg[si]])
                      else:
                          out_dma(npp[l, :, :], stage[113:128, si, :], [stg[si]])

              def c_pool(i, p=p, spp=spp, ppv=ppv):
                  c0, n = tile_cols(i)
                  cur_ap, cur_b, prev = stC[i]
                  bd = bank2()
                  mms = []
                  rd = [cur_b, cst2]
                  for cc in range(8):
                      g = cc // 2
                      o = ps[:, bd * 512 + cc * 128: bd * 512 + cc * 128 + n]
                      if i < NT128:
                          first_tile = (i == 0 and p == 0)
                          kind = 1 if first_tile else 0
                          mms.append((o, cur_ap[:, cc * 128:(cc + 1) * 128], bands[:, g * 3 + kind, :],
                                      True, first_tile))
                          if not first_tile:
                              mms.append((o, prev[0][:, cc * 128:(cc + 1) * 128], bands[:, g * 3 + 2, :],
                                          False, True))
                      else:
                          mms.append((o, ppv[:, 0, cc * 128:(cc + 1) * 128], selb[:, 0, g, :], True, False))
                          mms.append((o, ppv[:, 1, cc * 128:(cc + 1) * 128], selb[:, 1, g, :], False, False))
                          mms.append((o, cur_ap[:, cc * 128:(cc + 1) * 128], i8wb[:, g, :], False, True))
                  if i < NT128 and not (i == 0 and p == 0):
                      rd.append(prev[1])
                  if i == 8:
                      rd.append(slotb[spp])
                  pe_group(mms, rd, [bank[bd], bank[bd + 1]])
                  di = state["dt"] % 2
                  state["dt"] += 1
                  stC[i] = stC[i] + (di,)
                  act(Dt[:, di, :, 0:n], ps[:, bd * 512:(bd + 2) * 512].rearrange("p (c t) -> p c t", c=8)[:, :, 0:n],
                      AF.Copy, [bank[bd], bank[bd + 1]], [Dtb[di]])

              def c_lin(i, l=l, swp=swp):
                  c0, n = tile_cols(i)
                  di = stC[i][3]
                  bh = bank2()
                  mms = []
                  for dj in range(8):
                      g = dj // 2
                      o = ps[:, bh * 512 + dj * 128: bh * 512 + dj * 128 + n]
                      for kk in range(2):
                          mms.append((o, ring[:, swp, g * 2 + kk, (dj % 2) * 128:(dj % 2 + 1) * 128],
                                      Dt[:, di, g * 2 + kk, 0:n], kk == 0, kk == 1))
                  pe_group(mms, [Dtb[di], slotb[swp]], [bank[bh], bank[bh + 1]])
                  w = i // 4 if i < NT128 else 2
                  for dj in range(8):
                      act(hv[:, dj, c0:c0 + n], ps[:, bh * 512 + dj * 128: bh * 512 + dj * 128 + n], AF.Identity,
                          [bank[bh], bank[bh + 1], cst], [HC[(dj, w)]], bias=0.0, scale=pv(l, PV_PS, dj))

              run_pipeline(NTI, [c_proj, c_pool, c_lin], [0, 1, 2])

              _ck(9)
              merge_branch(l, 2, HC, w_pc, False, MG)

              _ck(10)
              mg = B1[:, 0:8 * NTOK].rearrange("p (c t) -> p c t", c=8)
              slots = [load_dunit(w_o[l, :, h * 512:(h + 1) * 512]) for h in range(2)]
              def pre_wo(i, acc, accb, slots=slots):
                  c0, n = tile_cols(i)
                  w = i // 4 if i < NT128 else 2
                  b = xstat_project(i, slots, [MG[(j, w)] for j in range(8)], lambda kc, c0=c0: mg[:, kc, c0:c0 + 128])
                  dve_stt(R[:, i, :], R[:, i, :], ALPHA, ps[:, b * 512:(b + 2) * 512],
                          ALU.mult, ALU.add, [Rb[i], bank[b], bank[b + 1]], [Rb[i], accb], accum_out=acc)
              av = B1[:, 0:8 * NTOK].rearrange("p (c t) -> p c t", c=8)
              fill = {"AG": None, "units": {}, "next": 0, "done": set()}

              def ff1_group(gq_, j, w, s1, AG_):
                  t0, n = cur["WT"][w]
                  jj = j % 2
                  bb = bank1()
                  pe_group([(pcols(bb, 0, n), ring[:, s1, kc, jj * 128:(jj + 1) * 128], T[:, kc, t0:t0 + n],
                             kc == 0, kc == 7) for kc in range(8)],
                           [slotb[s1]] + [Tb[i] for i in wt_tiles(w)], [bank[bb]])
                  f = ft_next()
                  act(ftmp[:, f, 0:n], pcols(bb, 0, n), AF.Relu, [bank[bb]], [ftb[f]])
                  dve_tt(av[:, j, t0:t0 + n], ftmp[:, f, 0:n], pcols(bb, 0, n), ALU.mult,
                         [ftb[f], bank[bb]], [AG_[(j, w)]])

              def filler(i, l=l):
                  if fill["AG"] is None:
                      dF0 = Sched.retag(ball_ref[0])
                      fill["AG"] = {(j, w): Buf(f"ag0_{j}_{w}", dF0) for j in range(8) for w in range(3)}
                      ball_ref[0] = list(fill["AG"].values())
                  for _ in range(2):
                      j = fill["next"]
                      if j >= 6:
                          return
                      jp = j // 2
                      if jp not in fill["units"]:
                          fill["units"][jp] = load_unit(w_f1[l, :, jp * 256:(jp + 1) * 256])
                      ff1_group(0, j, 0, fill["units"][jp], fill["AG"])
                      fill["done"].add((j, 0))
                      fill["next"] += 1

              ball_ref = [B1all]
              ln_pipeline(pre_wo, 0, transpose_R_to_T, extra=[(filler, NTI)])
              B1all = ball_ref[0]

              _ck(11)
              for gq in range(4):
                  if gq == 0 and fill["AG"] is not None:
                      AG = fill["AG"]
                  else:
                      dF = Sched.retag(B1all)
                      AG = {(j, w): Buf(f"ag{gq}_{j}_{w}", dF) for j in range(8) for w in range(3)}
                      B1all = list(AG.values())
                  for jp in range(4):
                      if gq == 0 and jp in fill["units"]:
                          s1 = fill["units"][jp]
                      else:
                          s1 = load_unit(w_f1[l, :, gq * 1024 + jp * 256: gq * 1024 + (jp + 1) * 256])
                      for jj in range(2):
                          j = jp * 2 + jj
                          for w, (t0, n) in enumerate(cur["WT"]):
                              if gq == 0 and (j, w) in fill["done"]:
                                  continue
                              ff1_group(gq, j, w, s1, AG)
                  slots = [load_dunit(w_f2[l, gq * 1024:(gq + 1) * 1024, h * 512:(h + 1) * 512]) for h in range(2)]
                  def pre_ff2(i, acc=None, accb=None, slots=slots, AG=AG, gq=gq):
                      c0, n = tile_cols(i)
                      w = i // 4 if i < NT128 else 2
                      b = xstat_project(i, slots, [AG[(j, w)] for j in range(8)], lambda kc, c0=c0: av[:, kc, c0:c0 + 128])
                      if gq == 0:
                          dve_stt(R[:, i, :], R[:, i, :], ALPHA, ps[:, b * 512:(b + 2) * 512],
                                  ALU.mult, ALU.add, [Rb[i], bank[b], bank[b + 1]], [Rb[i]])
                      elif acc is None:
                          dve_tt(R[:, i, :], R[:, i, :], ps[:, b * 512:(b + 2) * 512], ALU.add,
                                 [Rb[i], bank[b], bank[b + 1]], [Rb[i]])
                      else:
                          dve_stt(R[:, i, :], ps[:, b * 512:(b + 2) * 512], 1.0, R[:, i, :], ALU.mult, ALU.add,
                                  [Rb[i], bank[b], bank[b + 1]], [Rb[i], accb], accum_out=acc)

                  def post_ln2(i, l=l, tok_base=tok_base, smp_base=smp_base, p=p):
                      if l < DEPTH - 1:
                          transpose_R_to_T(i)
                      elif i < NT128:
                          out_dma(y_p[tok_base + i * 128: tok_base + (i + 1) * 128, :], R[:, i, :], [Rb[i]])
                          if p + 1 < NPASS:
                              nb_ = tok_base + PT
                              dma_sp(R[:, i, :], x_p[nb_ + i * 128: nb_ + (i + 1) * 128, :], writes=[Rb[i]])
                      else:
                          out_dma(y_s[smp_base:smp_base + SP_, :], R[0:SP_, 8, :], [Rb[8]])

                  if gq < 3:
                      for i in range(NTI):
                          pre_ff2(i)
                  else:
                      ln_pipeline(pre_ff2, 2, post_ln2)
    except _Stop:
        pass

    sems = {}
    for e in Sched.COMPUTE:
        sems[e] = es.enter_context(nc.semaphore(f"prog_{e}"))
    for q in ("sp", "pool"):
        for i in range(Sched.NDMASEM):
            sems[("dma", q, i)] = es.enter_context(nc.semaphore(f"dma_{q}_{i}"))
    final_waits = {}
    for k, n in S.dma_cnt.items():
        if k[1] == "sp":
            final_waits[k] = 16 * n
    block = es.enter_context(nc.Block())
    S.emit_all(nc, block, sems, final_waits)
    es.close()
    return nc


_CACHE = {}


def _get_nc():
    if "nc" not in _CACHE:
        _CACHE["nc"] = build_nc()
    return _CACHE["nc"]


def kernel(x_prompt, x_sample, state_conv, state_pool, w_in, lnv_g, lnv_b, w_spatial, b_spatial,
           w_proj_a, conv_w, conv_b, w_proj_b, w_pool, pool_scale, w_proj_c, w_o,
           ln1_g, ln1_b, w_ff1, w_ff2, ln2_g, ln2_b):
    f = lambda a: np.ascontiguousarray(np.asarray(a, dtype=np.float32))
    x_prompt, x_sample, state_conv, state_pool = f(x_prompt), f(x_sample), f(state_conv), f(state_pool)
    vecs = [f(lnv_g), f(lnv_b), f(conv_w)[:, 0], f(conv_w)[:, 1], f(conv_w)[:, 2], f(conv_b), f(pool_scale)]
    pvec = np.stack([np.stack([v[l].reshape(8, 128).T for v in vecs], axis=1) for l in range(DEPTH)], axis=1)
    pvec = np.ascontiguousarray(pvec.reshape(128, DEPTH * NPV * 8))
    lnbc = np.stack([np.stack([np.broadcast_to(v[l][None, :], (128, D)) for v in
                               (f(ln1_g), f(ln1_b), f(ln2_g), f(ln2_b))]) for l in range(DEPTH)])
    lnbc = np.ascontiguousarray(lnbc)
    wsT = np.ascontiguousarray(f(w_spatial).transpose(0, 3, 1, 2).reshape(DEPTH, 128, 512))
    bsbc = np.ascontiguousarray(np.broadcast_to(f(b_spatial).reshape(DEPTH, 1, 512), (DEPTH, 128, 512)))
    ws00 = np.ascontiguousarray(np.broadcast_to(f(w_spatial)[:, None, :, 0, 0], (DEPTH, 128, 4)))
    bs0 = np.ascontiguousarray(np.broadcast_to(f(b_spatial)[:, None, :, 0], (DEPTH, 128, 4)))
    ident, mask, bands, sel, i8w = _host_consts()
    shared = {
        "w_in": f(w_in), "w_proj_a": f(w_proj_a), "w_proj_b": f(w_proj_b), "w_proj_c": f(w_proj_c),
        "w_o": f(w_o), "w_ff1": f(w_ff1), "w_ff2": f(w_ff2), "w_pool": f(w_pool),
        "pvec": pvec, "lnbc": lnbc, "wsT": wsT, "bsbc": bsbc, "ws00": ws00, "bs0": bs0,
        "c_ident": ident, "c_mask": mask, "c_bands": np.ascontiguousarray(bands.reshape(128, 12 * 128)),
        "c_sel": np.ascontiguousarray(sel.reshape(128, 2 * 4 * SP_)),
        "c_i8w": np.ascontiguousarray(i8w.reshape(128, 4 * SP_)),
    }
    in_maps = []
    for c in range(NCORE):
        m = dict(shared)
        m["x_p"] = np.ascontiguousarray(x_prompt[c])
        m["x_s"] = np.ascontiguousarray(x_sample[c * NSAMP:(c + 1) * NSAMP, 0, :])
        m["sconv"] = np.ascontiguousarray(state_conv[:, c * NSAMP:(c + 1) * NSAMP])
        m["spool"] = np.ascontiguousarray(state_pool[:, c * NSAMP:(c + 1) * NSAMP])
        in_maps.append(m)
    nc = _get_nc()
    res = run_bass_kernel_spmd(nc, in_maps, core_ids=list(range(NCORE)))
    rs = res.results
    y_prompt = np.stack([rs[c]["y_p"] for c in range(NCORE)]).astype(np.float32)
    y_sample = np.concatenate([rs[c]["y_s"] for c in range(NCORE)])[:, None, :].astype(np.float32)
    new_conv_prompt = np.stack([rs[c]["ncp"] for c in range(NCORE)], axis=1).astype(np.float32)
    new_pool_prompt = np.stack([rs[c]["npp"] for c in range(NCORE)], axis=1).astype(np.float32)
    new_conv_sample = np.concatenate([rs[c]["ncs"] for c in range(NCORE)], axis=1).astype(np.float32)
    new_pool_sample = np.concatenate([rs[c]["nps"] for c in range(NCORE)], axis=1).astype(np.float32)
    chunk_v_sample = np.concatenate([rs[c]["cvs"] for c in range(NCORE)], axis=1)[:, :, None, :].astype(np.float32)
    return (y_prompt, y_sample, new_conv_prompt, new_pool_prompt, new_conv_sample, new_pool_sample,
            chunk_v_sample)
```
